# Optimizing a Trainium2 kernel written in Bass

```python
import math
import jax, jax.numpy as jnp
from jax import lax
import numpy as np


D_MODEL = 1024
BATCH = 2
SEQ = 16384
DEPTH = 2

HEAD_DIM = 64
A_HEADS = 8
A_KV = 2
B_HEADS = 8
B_KV = 2
C_HEADS = 4
C_KV = 2
C_V_DIM = 2 * HEAD_DIM
WINDOW = 128
BLOCK = 128
GRID_W = 64
ROPE_THETA = 10000.0
NUM_BUCKETS = 32
MAX_DISTANCE = 128
N_BIAS_HEADS = B_HEADS + C_HEADS
D_FF = 2816
N_BRANCH = 3
BRANCH_WIDTH = 512
EPS = 1e-6

A_Q_W = A_HEADS * HEAD_DIM
A_KV_W = A_KV * HEAD_DIM
B_Q_W = B_HEADS * HEAD_DIM
B_KV_W = B_KV * HEAD_DIM
C_Q_W = C_HEADS * 2 * HEAD_DIM
C_K_W = C_KV * 2 * HEAD_DIM
C_V_W = C_KV * C_V_DIM
IN_SIZES = (A_Q_W, A_KV_W, A_KV_W, B_Q_W, B_KV_W, B_KV_W, C_Q_W, C_K_W, C_V_W, N_BRANCH * D_MODEL)
IN_COLS = sum(IN_SIZES)

kernel_name = "hybrid_gated_axial_window_diff_attn_encoder"


def rmsnorm(x, g):
    xf = x.astype(jnp.float32)
    r = lax.rsqrt(jnp.mean(xf * xf, axis=-1, keepdims=True) + EPS)
    return (xf * r * g.astype(jnp.float32)).astype(x.dtype)


def swiglu(h, w_in, w_out):
    g, u = jnp.split(h @ w_in, 2, axis=-1)
    return (jax.nn.silu(g) * u) @ w_out


def t5_bucket(rel):
    nb = NUM_BUCKETS // 2
    max_exact = nb // 2
    side = jnp.where(rel > 0, nb, 0)
    n = jnp.abs(rel)
    nf = jnp.maximum(n, 1).astype(jnp.float32)
    large = max_exact + (jnp.log(nf / max_exact) / math.log(MAX_DISTANCE / max_exact) * (nb - max_exact)).astype(jnp.int32)
    large = jnp.minimum(large, nb - 1)
    return side + jnp.where(n < max_exact, n, large)


def axial_rope_tables(seq):
    rows = seq // GRID_W
    row_ids = jnp.repeat(jnp.arange(rows), GRID_W).astype(jnp.float32)
    col_ids = jnp.tile(jnp.arange(GRID_W), rows).astype(jnp.float32)
    half = HEAD_DIM // 2
    freqs = ROPE_THETA ** (-jnp.arange(0, half, 2, dtype=jnp.float32) / half)
    ang_r = row_ids[:, None] * freqs
    ang_c = col_ids[:, None] * freqs
    return (jnp.cos(ang_r), jnp.sin(ang_r), jnp.cos(ang_c), jnp.sin(ang_c))


def rope_1d(x, cos, sin):
    x1, x2 = jnp.split(x, 2, axis=-1)
    c = cos[None, :, None, :]
    s = sin[None, :, None, :]
    return jnp.concatenate([x1 * c - x2 * s, x1 * s + x2 * c], axis=-1)


def axial_rope(x, tables):
    cos_r, sin_r, cos_c, sin_c = tables
    xf = x.astype(jnp.float32)
    half = HEAD_DIM // 2
    out = jnp.concatenate([rope_1d(xf[..., :half], cos_r, sin_r), rope_1d(xf[..., half:], cos_c, sin_c)], axis=-1)
    return out.astype(x.dtype)


def to_blocks(t):
    b, s = t.shape[:2]
    return jnp.moveaxis(t.reshape((b, s // BLOCK, BLOCK) + t.shape[2:]), 1, 0)


def mixer_a(q, k, v, qg, kg, rope):
    b, s = q.shape[:2]
    scale = HEAD_DIM ** -0.5
    q = axial_rope(rmsnorm(q, qg), rope)
    k = axial_rope(rmsnorm(k, kg), rope)
    qb = to_blocks(q.reshape(b, s, A_KV, A_HEADS // A_KV, HEAD_DIM) * scale)

    def one_block(qblk):
        sc = jnp.einsum('bqkgd,bskd->bkgqs', qblk, k).astype(jnp.float32)
        p = jax.nn.softmax(sc, axis=-1).astype(v.dtype)
        return jnp.einsum('bkgqs,bskd->bqkgd', p, v)

    o = lax.map(one_block, qb)
    return jnp.moveaxis(o, 0, 1).reshape(b, s, A_HEADS * HEAD_DIM)


def window_bias_and_mask(rel_bias, seq):
    nb = seq // BLOCK
    i = jnp.arange(BLOCK)[:, None]
    j = jnp.arange(3 * BLOCK)[None, :]
    rel = j - BLOCK - i
    in_window = jnp.abs(rel) <= WINDOW
    key_pos = jnp.arange(nb)[:, None] * BLOCK - BLOCK + jnp.arange(3 * BLOCK)[None, :]
    in_range = (key_pos >= 0) & (key_pos < seq)
    valid = in_window[None] & in_range[:, None, :]
    bias = rel_bias[:, :B_HEADS][t5_bucket(rel)].astype(jnp.float32)
    bias = bias.reshape(BLOCK, 3 * BLOCK, B_KV, B_HEADS // B_KV).transpose(2, 3, 0, 1)
    return bias, valid


def mixer_b(q, k, v, sink, bias, valid):
    b, s = q.shape[:2]
    nb = s // BLOCK
    scale = HEAD_DIM ** -0.5
    pad = ((0, 0), (BLOCK, BLOCK), (0, 0), (0, 0))
    kp = jnp.pad(k, pad).reshape(b, nb + 2, BLOCK, B_KV, HEAD_DIM)
    vp = jnp.pad(v, pad).reshape(b, nb + 2, BLOCK, B_KV, HEAD_DIM)
    kband = jnp.concatenate([kp[:, :-2], kp[:, 1:-1], kp[:, 2:]], axis=2)
    vband = jnp.concatenate([vp[:, :-2], vp[:, 1:-1], vp[:, 2:]], axis=2)
    qb = q.reshape(b, nb, BLOCK, B_KV, B_HEADS // B_KV, HEAD_DIM) * scale
    sc = jnp.einsum('bnqkgd,bnskd->bnkgqs', qb, kband).astype(jnp.float32) + bias[None, None]
    sc = jnp.where(valid[None, :, None, None], sc, -1e30)
    sk = sink.astype(jnp.float32).reshape(B_KV, B_HEADS // B_KV)[None, None, :, :, None, None]
    m = jnp.maximum(jnp.max(sc, axis=-1, keepdims=True), sk)
    e = jnp.exp(sc - m)
    p = e / (jnp.sum(e, axis=-1, keepdims=True) + jnp.exp(sk - m))
    o = jnp.einsum('bnkgqs,bnskd->bnqkgd', p.astype(v.dtype), vband)
    return o.reshape(b, s, B_HEADS * HEAD_DIM)


def mixer_c(q, k, v, lq1, lk1, lq2, lk2, subln_g, lambda_init, rel_bias):
    b, s = q.shape[:2]
    nb = s // BLOCK
    g = C_HEADS // C_KV
    scale = HEAD_DIM ** -0.5
    f32 = jnp.float32
    lam = (jnp.exp(jnp.sum(lq1.astype(f32) * lk1.astype(f32))) - jnp.exp(jnp.sum(lq2.astype(f32) * lk2.astype(f32))) + lambda_init)
    table = rel_bias[:, B_HEADS:]
    qb = to_blocks(q.reshape(b, s, C_KV, g, 2, HEAD_DIM) * scale)
    starts = jnp.arange(nb) * BLOCK
    key_pos = jnp.arange(s)

    def one_block(args):
        qblk, q0 = args
        sc = jnp.einsum('bqkgmd,bskmd->bkgmqs', qblk, k).astype(f32)
        rel = key_pos[None, :] - (q0 + jnp.arange(BLOCK))[:, None]
        bias = table[t5_bucket(rel)].astype(f32).reshape(BLOCK, s, C_KV, g).transpose(2, 3, 0, 1)
        p = jax.nn.softmax(sc + bias[None, :, :, None], axis=-1)
        attn = p[:, :, :, 0] - lam * p[:, :, :, 1]
        return jnp.einsum('bkgqs,bskd->bqkgd', attn.astype(v.dtype), v)

    o = lax.map(one_block, (qb, starts))
    o = jnp.moveaxis(o, 0, 1).reshape(b, s, C_HEADS, C_V_DIM)
    o = rmsnorm(o, subln_g) * (1.0 - lambda_init)
    return o.reshape(b, s, C_HEADS * C_V_DIM)


def setup_inputs(seed: int = 0) -> dict:
    key = jax.random.key(seed)
    ks = jax.random.split(key, 24)
    f32 = jnp.float32
    nrm = lambda k, shape, sc: jax.random.normal(k, shape, f32) * sc
    gain = lambda k, shape: 1.0 + 0.02 * jax.random.normal(k, shape, f32)
    return {
        "x": jax.random.normal(ks[0], (BATCH, SEQ, D_MODEL), f32),
        "rel_bias": nrm(ks[1], (NUM_BUCKETS, N_BIAS_HEADS), 0.5),
        "norm_ffn1": gain(ks[2], (DEPTH, D_MODEL)),
        "w_ffn1_in": nrm(ks[3], (DEPTH, D_MODEL, 2 * D_FF), D_MODEL ** -0.5),
        "w_ffn1_out": nrm(ks[4], (DEPTH, D_FF, D_MODEL), D_FF ** -0.5),
        "norm_mix": gain(ks[5], (DEPTH, D_MODEL)),
        "w_in": nrm(ks[6], (DEPTH, D_MODEL, IN_COLS), D_MODEL ** -0.5),
        "qnorm_a": gain(ks[7], (DEPTH, HEAD_DIM)),
        "knorm_a": gain(ks[8], (DEPTH, HEAD_DIM)),
        "sink_b": nrm(ks[9], (DEPTH, B_HEADS), 0.5),
        "lam_q1": nrm(ks[10], (DEPTH, HEAD_DIM), 0.1),
        "lam_k1": nrm(ks[11], (DEPTH, HEAD_DIM), 0.1),
        "lam_q2": nrm(ks[12], (DEPTH, HEAD_DIM), 0.1),
        "lam_k2": nrm(ks[13], (DEPTH, HEAD_DIM), 0.1),
        "subln_c": gain(ks[14], (DEPTH, C_V_DIM)),
        "w_branch": nrm(ks[15], (DEPTH, N_BRANCH, BRANCH_WIDTH, D_MODEL), BRANCH_WIDTH ** -0.5),
        "w_out": nrm(ks[16], (DEPTH, D_MODEL, D_MODEL), D_MODEL ** -0.5),
        "norm_ffn2": gain(ks[17], (DEPTH, D_MODEL)),
        "w_ffn2_in": nrm(ks[18], (DEPTH, D_MODEL, 2 * D_FF), D_MODEL ** -0.5),
        "w_ffn2_out": nrm(ks[19], (DEPTH, D_FF, D_MODEL), D_FF ** -0.5),
        "norm_final": gain(ks[20], (D_MODEL,)),
    }


def reference(x, rel_bias, norm_ffn1, w_ffn1_in, w_ffn1_out, norm_mix, w_in, qnorm_a, knorm_a, sink_b, lam_q1, lam_k1, lam_q2, lam_k2, subln_c, w_branch, w_out, norm_ffn2, w_ffn2_in, w_ffn2_out, norm_final):
    b, s, _ = x.shape
    rope = axial_rope_tables(s)
    win_bias, win_valid = window_bias_and_mask(rel_bias, s)
    split_points = np.cumsum(IN_SIZES)[:-1].tolist()
    for l in range(DEPTH):
        lambda_init = 0.8 - 0.6 * math.exp(-0.3 * l)
        x = x + 0.5 * swiglu(rmsnorm(x, norm_ffn1[l]), w_ffn1_in[l], w_ffn1_out[l])
        h = rmsnorm(x, norm_mix[l])
        qa, ka, va, qb_, kb, vb, qc, kc, vc, gate_logits = jnp.split(h @ w_in[l], split_points, axis=-1)
        y_a = mixer_a(qa.reshape(b, s, A_HEADS, HEAD_DIM), ka.reshape(b, s, A_KV, HEAD_DIM), va.reshape(b, s, A_KV, HEAD_DIM), qnorm_a[l], knorm_a[l], rope)
        y_b = mixer_b(qb_.reshape(b, s, B_HEADS, HEAD_DIM), kb.reshape(b, s, B_KV, HEAD_DIM), vb.reshape(b, s, B_KV, HEAD_DIM), sink_b[l], win_bias, win_valid)
        y_c = mixer_c(qc.reshape(b, s, C_HEADS, 2, HEAD_DIM), kc.reshape(b, s, C_KV, 2, HEAD_DIM), vc.reshape(b, s, C_KV, C_V_DIM), lam_q1[l], lam_k1[l], lam_q2[l], lam_k2[l], subln_c[l], lambda_init, rel_bias)
        ys = jnp.stack([y_a, y_b, y_c], axis=2)
        branches = jnp.einsum('bsnw,nwd->bsnd', ys, w_branch[l])
        gates = jax.nn.sigmoid(gate_logits.reshape(b, s, N_BRANCH, D_MODEL))
        merged = jnp.sum(gates * branches, axis=2)
        x = x + merged @ w_out[l]
        x = x + 0.5 * swiglu(rmsnorm(x, norm_ffn2[l]), w_ffn2_in[l], w_ffn2_out[l])
    return rmsnorm(x, norm_final)
```

```python
import math
from contextlib import ExitStack

import numpy as np
import ml_dtypes

import concourse.bass as bass
import concourse.mybir as mybir
from concourse.bass_utils import run_bass_kernel_spmd

F32 = mybir.dt.float32
BF16 = mybir.dt.bfloat16
ALU = mybir.AluOpType
AF = mybir.ActivationFunctionType
NPBF = ml_dtypes.bfloat16

D = 1024
DFF = 2816
NFF = 22
EPS = 1e-6
NEG = -30000.0
NCORES = 8
TT = 1024


class Buf:
    __slots__ = ("w", "r")

    def __init__(self):
        self.w = []
        self.r = []


class Op:
    __slots__ = ("eng", "fn", "deps", "sem", "val", "signal", "is_dma")


class Prog:
    ENGS = ("pe", "act", "dve", "pool", "sp")
    NDSEM = 12

    def __init__(self, nc, es):
        self.nc = nc
        self.ops = {e: [] for e in self.ENGS}
        self.csem = {e: es.enter_context(nc.semaphore("cs_" + e)) for e in self.ENGS}
        self.dsem = {}
        self.dcount = {}
        self.dhist = {}
        for e in ("pool", "sp", "conv"):
            self.dsem[e] = [es.enter_context(nc.semaphore("ds_%s%d" % (e, i))) for i in range(self.NDSEM)]
            self.dcount[e] = 0
            self.dhist[e] = []
        self.out_dmas = []
        self.ccsem = es.enter_context(nc.semaphore("cc_sem"))
        self.cccount = 0
        self.cchist = []
        self.fence = []
        self.need_fence = {e: False for e in self.ENGS}

    def barrier(self):
        f = []
        for e in self.ENGS:
            comp = [o for o in self.ops[e] if not o.is_dma]
            if comp:
                f.append(comp[-1])
        for e in ("pool", "sp", "conv"):
            f.extend(self.dhist[e][-self.NDSEM:])
        f.extend(self.cchist[-1:])
        self.fence = f
        for e in self.ENGS:
            self.need_fence[e] = True

    def _mk(self, eng, fn, reads, writes, deps):
        o = Op()
        o.eng = eng
        o.fn = fn
        o.sem = None
        o.val = None
        o.signal = False
        o.is_dma = False
        dl = list(deps)
        if self.need_fence[eng]:
            dl.extend(self.fence)
            self.need_fence[eng] = False
        for b in reads:
            dl.extend(b.w)
        for b in writes:
            dl.extend(b.r)
            dl.extend(b.w)
        o.deps = dl
        self.ops[eng].append(o)
        return o

    def _upd(self, o, reads, writes):
        for b in writes:
            b.w = [o]
            b.r = []
        for b in reads:
            if o.is_dma:
                b.r.append(o)
            else:
                b.r = [x for x in b.r if x.is_dma or x.eng != o.eng]
                b.r.append(o)

    def run(self, eng, fn, reads=(), writes=(), deps=()):
        o = self._mk(eng, fn, reads, writes, deps)
        self._upd(o, reads, writes)
        return o

    def dma(self, eng, out, in_, reads=(), writes=(), deps=(), ring=None):
        o = self._mk(eng, lambda e: e.dma_start(out=out, in_=in_), reads, writes, deps)
        o.is_dma = True
        rk = eng if ring is None else ring
        k = self.dcount[rk]
        self.dcount[rk] = k + 1
        o.sem = self.dsem[rk][k % self.NDSEM]
        o.val = 16 * (k // self.NDSEM + 1)
        if k >= self.NDSEM:
            o.deps.append(self.dhist[rk][k - self.NDSEM])
        self.dhist[rk].append(o)
        self._upd(o, reads, writes)
        return o

    def coll(self, fn, reads=(), writes=()):
        o = self._mk("pool", fn, reads, writes, ())
        o.is_dma = True
        o.signal = "cc"
        self.cccount += 1
        o.sem = self.ccsem
        o.val = self.cccount
        self.cchist.append(o)
        self._upd(o, reads, writes)
        return o

    def replay(self, block):
        for e in self.ENGS:
            for o in self.ops[e]:
                for d in o.deps:
                    if not d.is_dma:
                        if d.eng == "pe" and o.eng == "pe":
                            continue
                        d.signal = True
        for e in self.ENGS:
            c = 0
            for o in self.ops[e]:
                if (not o.is_dma) and o.signal:
                    c += 1
                    o.sem = self.csem[e]
                    o.val = c

        finals = list(self.out_dmas)
        for e in ("pool", "sp", "conv"):
            finals.extend(self.dhist[e][-self.NDSEM:])

        def go(ename, eng, tail=False):
            waited = {}
            for o in self.ops[ename]:
                for d in o.deps:
                    if (not d.is_dma) and d.eng == "pe" and ename == "pe":
                        continue
                    key = id(d.sem)
                    if waited.get(key, 0) >= d.val:
                        continue
                    eng.wait_ge(d.sem, d.val)
                    waited[key] = d.val
                ins = o.fn(eng)
                if o.signal == "cc":
                    ins.then_inc(o.sem)
                elif o.is_dma:
                    ins.then_inc(o.sem, 16)
                elif o.signal:
                    ins.then_inc(o.sem, 1)
            if tail:
                for d in finals:
                    key = id(d.sem)
                    if waited.get(key, 0) >= d.val:
                        continue
                    eng.wait_ge(d.sem, d.val)
                    waited[key] = d.val

        @block.tensor
        def _(t):
            go("pe", t)

        @block.scalar
        def _(s):
            go("act", s)

        @block.vector
        def _(v):
            go("dve", v)

        @block.gpsimd
        def _(g):
            go("pool", g)

        @block.sync
        def _(s):
            go("sp", s, tail=True)


class Tile:
    def __init__(self, t, *dims):
        self.t = t
        self.dims = dims
        n = 1
        for d in dims:
            n *= d
        self.bufs = [Buf() for _ in range(max(n, 1))]

    def b(self, *idx):
        k = 0
        for i, d in zip(idx, self.dims):
            k = k * d + i
        return self.bufs[k]

    def all(self):
        return self.bufs


class Builder:
    def __init__(self, T, stages, ext_in, ext_out, last_layer_final, fused=False):
        self.fused = fused
        self.T = T
        self.S = 4 * T
        self.NT = T // 128
        self.NKT = self.S // 128
        self.NQ = T // 512
        self.NTT = T // TT
        self.stages = stages
        self.ext_in = set(ext_in)
        self.ext_out = set(ext_out)
        self.final = last_layer_final
        self.nc = bass.Bass("TRN2", target_bir_lowering=False, num_devices=NCORES)
        self.dr = {}
        self.drb = {}

    WSHAPES = {"wf1i": [NFF, 128, 8, 256], "wf1o": [8, 128, NFF, 128], "wfm": [40, 128, 8, 128], "wv": [1, 128, 8, 512],
               "wbr": [8, 128, 12, 128], "wo": [8, 128, 8, 128], "wf2i": [NFF, 128, 8, 256], "wf2o": [8, 128, NFF, 128]}

    def emit_conv(self, l):
        for base in ("wf1i", "wf1o", "wfm", "wv", "wbr", "wo", "wf2i", "wf2o"):
            shp = self.WSHAPES[base]
            nm = "%s%d" % (base, l)
            src = self.dram(nm, shp if base != "wv" else [128, 8, 512], F32)
            dst = self.dram(nm + "b", shp if base != "wv" else [128, 8, 512], BF16)
            if base == "wv":
                s2, d2 = src.rearrange("p k c -> p (k c)"), dst.rearrange("p k c -> p (k c)")
            else:
                s2, d2 = src.rearrange("j p k c -> (j p) (k c)"), dst.rearrange("j p k c -> (j p) (k c)")
            self.P.dma("pool", d2, s2, writes=[self.dbuf(nm + "b")], ring="conv")

    def wdram(self, base, l):
        shp = self.WSHAPES[base]
        if base == "wv":
            shp = [128, 8, 512]
        nm = "%s%d" % (base, l)
        if self.fused and PRECONVERT:
            return self.dram(nm + "b", shp, BF16), [self.dbuf(nm + "b")]
        return self.dram(nm, shp, F32), []

    def sbt(self, name, shape, dt):
        self._uid = getattr(self, "_uid", 0) + 1
        return self.nc.sbuf_tensor("%s_u%d" % (name, self._uid), shape, dt)

    def dram(self, name, shape, dt):
        if name in self.dr:
            return self.dr[name]
        if name in self.ext_in:
            kind = "ExternalInput"
        elif name in self.ext_out:
            kind = "ExternalOutput"
        else:
            kind = "Internal"
        t = self.nc.dram_tensor(name, list(shape), dt, kind=kind).ap()
        self.dr[name] = t
        return t

    def dbuf(self, name, *idx):
        key = (name,) + idx
        if key not in self.drb:
            self.drb[key] = Buf()
        return self.drb[key]

    def build(self):
        nc = self.nc
        T, S = self.T, self.S
        with ExitStack() as es:
            self.P = P = Prog(nc, es)
            self.ps = es.enter_context(nc.psum_tensor("ps", [128, 8, 512], F32))
            self.bank = [Buf() for _ in range(8)]
            self.bank_rr = 0
            self.ones = es.enter_context(self.sbt("ones", [128, 128], BF16))
            self.blk = es.enter_context(self.sbt("blk", [128, 128], BF16))
            self.rmat = es.enter_context(self.sbt("sb_rmat", [128, 128], BF16))
            self.cB = Buf()
            P.run("pool", lambda e: e.memset(self.ones[:], 1.0), writes=[self.cB])
            P.run("pool", lambda e: e.memset(self.blk[:], 0.0), writes=[self.cB])
            P.run("pool", lambda e: e.memset(self.blk[0:64, 0:64], 1.0), writes=[self.cB])
            P.run("pool", lambda e: e.memset(self.blk[64:128, 64:128], 1.0), writes=[self.cB])
            rm = self.dram("rmat", [128, 128], F32)
            P.dma("pool", self.rmat[:], rm, writes=[self.cB])
            self.ident = es.enter_context(self.sbt("sb_ident", [128, 128], BF16))
            P.dma("pool", self.ident[:], self.dram("ident", [128, 128], F32), writes=[self.cB])
            self.vec = {}
            for l in sorted(set(s[1] for s in self.stages)):
                vt = es.enter_context(self.sbt("sb_vec%d" % l, [128, 320], F32))
                vd = self.dram("vec%d" % l, [128, 320], F32)
                P.dma("sp", vt[:], vd, writes=[self.cB])
                self.vec[l] = vt
            if self.final:
                self.gfin = es.enter_context(self.sbt("sb_gfin", [128, 8], F32))
                P.dma("sp", self.gfin[:], self.dram("gfin", [128, 8], F32), writes=[self.cB])
            if self.fused and PRECONVERT:
                self.emit_conv(0)
            for st in self.stages:
                kind, l = st
                if self.fused and PRECONVERT and kind == "attn" and l == 0:
                    self.emit_conv(1)
                if kind == "d1":
                    P.barrier()
                    self.stage_dense1(l)
                elif kind == "attn":
                    P.barrier()
                    self.stage_attn_a(l)
                    P.barrier()
                    self.stage_attn_c(l)
                    P.barrier()
                    self.stage_attn_b(l)
                elif kind == "d2":
                    P.barrier()
                    self.stage_dense2(l)
                elif kind == "ag":
                    self.stage_gather(l)
            block = es.enter_context(nc.Block())
            P.replay(block)
        return nc

    def seg_k(self, l, which, j, kv=0):
        T = self.T
        if self.fused:
            rb = self.dram("rb%d" % l, [4, 2 * self.NTT, 512, 1024], BF16)
            r0 = {"a": 0, "b": 128, "c": 256 + kv * 128}[which]
            return rb[j, 0:self.NTT].rearrange("t r c -> r t c")[r0:r0 + 128]
        if which == "a":
            a = self.dram("kaf%d" % l, [128, self.S], BF16)[:, j * T:(j + 1) * T]
        elif which == "b":
            a = self.dram("kbf%d" % l, [128, self.S], BF16)[:, j * T:(j + 1) * T]
        else:
            a = self.dram("kcf%d" % l, [128, 2, self.S], BF16)[:, kv, j * T:(j + 1) * T]
        return a.rearrange("p (t c) -> p t c", c=1024)

    def seg_v(self, l, j):
        T = self.T
        if self.fused:
            rb = self.dram("rb%d" % l, [4, 2 * self.NTT, 512, 1024], BF16)
            return rb[j, self.NTT:2 * self.NTT].rearrange("t r (a c) -> (t r a) c", c=512)
        return self.dram("vf%d" % l, [self.S, 512], BF16)[j * T:(j + 1) * T, :]

    def kv_rbuf(self, l):
        if self.fused:
            return [self.dbuf("rb%d" % l)]
        return [self.dbuf("kvext%d" % l)]

    def gather_tile(self, l, tt):
        P, NTT = self.P, self.NTT
        rg = [[0, 1, 2, 3], [4, 5, 6, 7]]
        own = self.dram("kvo%d" % l, [2 * NTT, 512, 1024], BF16)
        gat = self.dram("kvg%d" % l, [2 * NTT, 2048, 1024], BF16)
        for n in (tt, NTT + tt):
            P.coll(lambda e, n=n: e.collective_compute("AllGather", ALU.bypass, replica_groups=rg,
                                                       ins=[own[n].opt()], outs=[gat[n].opt()]),
                   reads=[self.dbuf("kvo%d" % l, tt)], writes=[self.dbuf("kvg%d" % l, n)])

    def stage_gather(self, l):
        P, T, NTT = self.P, self.T, self.NTT
        rg = [[0, 1, 2, 3], [4, 5, 6, 7]]
        own = self.dram("kvo%d" % l, [2 * NTT, 512, 1024], BF16)
        gat = self.dram("kvg%d" % l, [2 * NTT, 2048, 1024], BF16)
        rb = self.dram("rb%d" % l, [4, 2 * NTT, 512, 1024], BF16)
        for j in range(4):
            self.rot_dma(rb[j], gat, 512, j, lambda a: a, [self.dbuf("kvg%d" % l, n) for n in range(2 * NTT)],
                         [self.dbuf("rb%d" % l)])

    def rot_dma(self, out, src, rows, j, sel, reads, writes):
        def fn(e):
            cache = self.__dict__.setdefault("_rotv", {})
            if (j, rows) not in cache:
                v = e.partition_id()
                cache[(j, rows)] = e.snap(((v % 4 + j) % 4) * rows, min_val=0, max_val=3 * rows)
            w = cache[(j, rows)]
            return e.dma_start(out=out, in_=sel(src[:, bass.ds(w, rows), :]))
        o = self.P._mk("sp", fn, reads, writes, ())
        P = self.P
        o.is_dma = True
        k = P.dcount["sp"]
        P.dcount["sp"] = k + 1
        o.sem = P.dsem["sp"][k % P.NDSEM]
        o.val = 16 * (k // P.NDSEM + 1)
        if k >= P.NDSEM:
            o.deps.append(P.dhist["sp"][k - P.NDSEM])
        P.dhist["sp"].append(o)
        P._upd(o, reads, writes)
        return o

    def nbank(self):
        k = self.bank_rr
        self.bank_rr = (k + 1) % 8
        return k

    def alloc_dense(self, es, l, with_y):
        nc = self.nc
        d = {}
        d["x"] = Tile(es.enter_context(self.sbt("xt", [128, 8, TT], F32)), 8, 2)
        d["h"] = Tile(es.enter_context(self.sbt("hT", [128, 8, TT], BF16)), 8, 2)
        d["a"] = Tile(es.enter_context(self.sbt("aT", [128, NFF, TT], BF16)), NFF, 2)
        d["w"] = [Tile(es.enter_context(self.sbt("ws%d" % i, [128, 4096], BF16))) for i in range(4)]
        d["wi"] = 0
        d["sq"] = Tile(es.enter_context(self.sbt("sq", [128, 8, 512], BF16)))
        d["rr"] = [Tile(es.enter_context(self.sbt("rr%d" % i, [128, 512], F32))) for i in range(2)]
        d["sg"] = [Tile(es.enter_context(self.sbt("sg%d" % i, [128, 512], F32))) for i in range(2)]
        d["sgi"] = 0
        d["ost"] = [Tile(es.enter_context(self.sbt("ost%d" % i, [128, 512], BF16))) for i in range(6)]
        d["osti"] = 0
        d["f1"] = [Tile(es.enter_context(self.sbt("f1_%d" % i, [128, 512], F32))) for i in range(4)]
        d["f1i"] = 0
        d["b1"] = [Tile(es.enter_context(self.sbt("b1_%d" % i, [128, 512], BF16))) for i in range(4)]
        d["b1i"] = 0
        if with_y:
            d["y"] = Tile(es.enter_context(self.sbt("yT", [128, 12, TT], BF16)))
            d["g"] = [Tile(es.enter_context(self.sbt("gr%d" % i, [128, 3, TT], BF16))) for i in range(2)]
        else:
            d["cs"] = Tile(es.enter_context(self.sbt("cst", [128, 2, TT], F32)))
        return d

    def ring(self, d, key):
        lst = d[key]
        i = d[key + "i"] if (key + "i") in d else 0
        d[key + "i"] = (i + 1) % len(lst)
        return lst[i]

    def wslot(self, d):
        i = d["wi"]
        d["wi"] = (i + 1) % len(d["w"])
        return d["w"][i]

    def mm_group(self, bank, n, lhs_fn, rhs_fn, reads):
        P = self.P
        ps = self.ps
        for k in range(n):
            lhsT = lhs_fn(k)
            rhs = rhs_fn(k)
            P.run("pe", (lambda e, lhsT=lhsT, rhs=rhs, k=k: e.matmul(ps[:, bank, :], lhsT, rhs, start=(k == 0), stop=(k == n - 1))),
                  reads=reads if (k == 0 or k == n - 1) else (), writes=[self.bank[bank]])

    def rmsnorm(self, d, gap, dst_fn=None):
        P = self.P
        ps = self.ps
        X, H, SQ = d["x"], d["h"], d["sq"]
        for sb in range(2):
            sl = slice(sb * 512, (sb + 1) * 512)
            P.run("act", lambda e, sl=sl: e.activation(out=SQ.t[:, :, :], in_=X.t[:, :, sl], func=AF.Square),
                  reads=[X.b(kc, sb) for kc in range(8)], writes=SQ.all())
            bk = self.nbank()
            self.mm_group(bk, 8, lambda k: self.ones[:, :], lambda k: SQ.t[:, k, :], reads=[self.cB] + SQ.all())
            rr = d["rr"][sb]
            P.run("act", lambda e, rr=rr, bk=bk: e.activation(out=rr.t[:, :], in_=ps[:, bk, :], func=AF.Sqrt, bias=EPS, scale=1.0 / D),
                  reads=[self.bank[bk]], writes=rr.all())
            P.run("dve", lambda e, rr=rr: e.reciprocal(rr.t[:, :], rr.t[:, :]), reads=rr.all(), writes=rr.all())
            for kc in range(8):
                if dst_fn is None:
                    out = H.t[:, kc, sl]
                    wr = [H.b(kc, sb)]
                else:
                    out = X.t[:, kc, sl]
                    wr = [X.b(kc, sb)]
                P.run("dve", lambda e, out=out, kc=kc, sl=sl, rr=rr: e.scalar_tensor_tensor(
                    out=out, in0=X.t[:, kc, sl], scalar=gap[:, kc:kc + 1], in1=rr.t[:, :], op0=ALU.mult, op1=ALU.mult),
                    reads=[X.b(kc, sb), self.cB] + rr.all(), writes=wr)

    def ffn(self, d, wi, wo):
        P = self.P
        ps = self.ps
        X, H, A = d["x"], d["h"], d["a"]
        wi, wi_r = wi
        wo, wo_r = wo
        for j in range(NFF):
            ws = self.wslot(d)
            P.dma("pool", ws.t[:, 0:2048], wi[j].rearrange("p k c -> p (k c)"), reads=wi_r, writes=ws.all())
            banks = {}
            for sb in range(2):
                for half in range(2):
                    bk = self.nbank()
                    banks[(sb, half)] = bk
                    self.mm_group(bk, 8,
                                  lambda k, half=half: ws.t[:, k * 256 + half * 128: k * 256 + half * 128 + 128],
                                  lambda k, sb=sb: H.t[:, k, sb * 512:(sb + 1) * 512],
                                  reads=ws.all() + [H.b(kc, sb) for kc in range(8)])
            for sb in range(2):
                sg = self.ring(d, "sg")
                bg, bu = banks[(sb, 0)], banks[(sb, 1)]
                P.run("act", lambda e, sg=sg, bg=bg: e.activation(out=sg.t[:, :], in_=ps[:, bg, :], func=AF.Silu),
                      reads=[self.bank[bg]], writes=sg.all())
                P.run("dve", lambda e, sg=sg, bu=bu, j=j, sb=sb: e.tensor_tensor(
                    A.t[:, j, sb * 512:(sb + 1) * 512], ps[:, bu, :], sg.t[:, :], ALU.mult),
                    reads=[self.bank[bu]] + sg.all(), writes=[A.b(j, sb)])
        for oc in range(8):
            ws = self.wslot(d)
            P.dma("pool", ws.t[:, 0:NFF * 128], wo[oc].rearrange("p k c -> p (k c)"), reads=wo_r, writes=ws.all())
            for sb in range(2):
                bk = self.nbank()
                self.mm_group(bk, NFF, lambda k: ws.t[:, k * 128:(k + 1) * 128],
                              lambda k, sb=sb: A.t[:, k, sb * 512:(sb + 1) * 512],
                              reads=ws.all() + [A.b(j, sb) for j in range(NFF)])
                sl = slice(sb * 512, (sb + 1) * 512)
                P.run("dve", lambda e, bk=bk, oc=oc, sl=sl: e.scalar_tensor_tensor(
                    out=X.t[:, oc, sl], in0=ps[:, bk, :], scalar=0.5, in1=X.t[:, oc, sl], op0=ALU.mult, op1=ALU.add),
                    reads=[self.bank[bk], X.b(oc, sb)], writes=[X.b(oc, sb)])

    def stage_dense1(self, l):
        nc, P, T = self.nc, self.P, self.T
        ps = self.ps
        xin = self.dram("xin%d" % l, [128, 8, T], F32)
        xres = self.dram("xres%d" % l, [128, 8, T], F32)
        wi = self.wdram("wf1i", l)
        wo = self.wdram("wf1o", l)
        wfm, wfm_r = self.wdram("wfm", l)
        wv, wv_r = self.wdram("wv", l)
        csd = self.dram("cs", [128, 2, T], F32)
        qd = [self.dram("q%s%d" % (m, l), [128, 4, T], BF16) for m in "abc"]
        if self.fused:
            kvo = self.dram("kvo%d" % l, [2 * self.NTT, 512, 1024], BF16)
            kad = kvo[0:self.NTT, 0:128, :].rearrange("t r c -> r t c")
            kbd = kvo[0:self.NTT, 128:256, :].rearrange("t r c -> r t c")
            kcd = kvo[0:self.NTT, 256:512, :].rearrange("t (k p) c -> p k t c", k=2)
            vd = kvo[self.NTT:2 * self.NTT].rearrange("t r (a c) -> (t r a) c", c=512)
        else:
            kad = self.dram("kao%d" % l, [128, T], BF16)
            kbd = self.dram("kbo%d" % l, [128, T], BF16)
            kcd = self.dram("kco%d" % l, [128, 2, T], BF16)
            vd = self.dram("vo%d" % l, [T, 512], BF16)
        gd = self.dram("gt%d" % l, [128, 24, T], BF16)
        vec = self.vec[l]
        with ExitStack() as es:
            d = self.alloc_dense(es, l, with_y=False)
            X, H = d["x"], d["h"]
            for tt in range(self.NTT):
                t0 = tt * TT
                if tt == 0:
                    P.dma("sp", X.t[:, :, :], xin[:, :, t0:t0 + TT], reads=[self.dbuf("xin%d" % l, tt)], writes=X.all())
                P.dma("sp", d["cs"].t[:, :, :], csd[:, :, t0:t0 + TT], writes=d["cs"].all())
                self.rmsnorm(d, vec[:, 0:8])
                self.ffn(d, wi, wo)
                P.dma("sp", xres[:, :, t0:t0 + TT], X.t[:, :, :], reads=X.all(), writes=[self.dbuf("xres%d" % l, tt)])
                self.rmsnorm(d, vec[:, 8:16])
                if tt + 1 < self.NTT:
                    P.dma("sp", X.t[:, :, :], xin[:, :, t0 + TT:t0 + 2 * TT], reads=[self.dbuf("xin%d" % l, tt + 1)], writes=X.all())
                for c in range(40):
                    ws = self.wslot(d)
                    P.dma("pool", ws.t[:, 0:1024], wfm[c].rearrange("p k c -> p (k c)"), reads=wfm_r, writes=ws.all())
                    for sb in range(2):
                        sl = slice(sb * 512, (sb + 1) * 512)
                        tsl = slice(t0 + sb * 512, t0 + (sb + 1) * 512)
                        bk = self.nbank()
                        self.mm_group(bk, 8, lambda k: ws.t[:, k * 128:(k + 1) * 128],
                                      lambda k, sl=sl: H.t[:, k, sl],
                                      reads=ws.all() + [H.b(kc, sb) for kc in range(8)])
                        ost = self.ring(d, "ost")
                        if c < 5:
                            isq = c < 4
                            sq1 = self.ring(d, "b1")
                            P.run("act", lambda e, sq1=sq1, bk=bk: e.activation(out=sq1.t[:, :], in_=ps[:, bk, :], func=AF.Square),
                                  reads=[self.bank[bk]], writes=sq1.all())
                            b2 = self.nbank()
                            self.mm_group(b2, 1, lambda k: self.blk[:, :], lambda k, sq1=sq1: sq1.t[:, :], reads=[self.cB] + sq1.all())
                            rt = self.ring(d, "f1")
                            if isq:
                                P.run("act", lambda e, rt=rt, b2=b2: e.activation(out=rt.t[:, :], in_=ps[:, b2, :], func=AF.Sqrt, bias=64.0 * EPS, scale=1.0),
                                      reads=[self.bank[b2]], writes=rt.all())
                            else:
                                P.run("act", lambda e, rt=rt, b2=b2: e.activation(out=rt.t[:, :], in_=ps[:, b2, :], func=AF.Sqrt, bias=EPS, scale=1.0 / 64.0),
                                      reads=[self.bank[b2]], writes=rt.all())
                            P.run("dve", lambda e, rt=rt: e.reciprocal(rt.t[:, :], rt.t[:, :]), reads=rt.all(), writes=rt.all())
                            xn = self.ring(d, "b1")
                            gcol = 24 if isq else 25
                            P.run("dve", lambda e, xn=xn, bk=bk, rt=rt, gcol=gcol: e.scalar_tensor_tensor(
                                out=xn.t[:, :], in0=ps[:, bk, :], scalar=vec[:, gcol:gcol + 1], in1=rt.t[:, :], op0=ALU.mult, op1=ALU.mult),
                                reads=[self.bank[bk], self.cB] + rt.all(), writes=xn.all())
                            b3 = self.nbank()
                            self.mm_group(b3, 1, lambda k: self.rmat[:, :], lambda k, xn=xn: xn.t[:, :], reads=[self.cB] + xn.all())
                            t1 = self.ring(d, "f1")
                            P.run("dve", lambda e, t1=t1, xn=xn, sl=sl: e.tensor_tensor(t1.t[:, :], xn.t[:, :], d["cs"].t[:, 0, sl], ALU.mult),
                                  reads=xn.all() + d["cs"].all(), writes=t1.all())
                            t2 = self.ring(d, "f1")
                            P.run("dve", lambda e, t2=t2, b3=b3, sl=sl: e.tensor_tensor(t2.t[:, :], ps[:, b3, :], d["cs"].t[:, 1, sl], ALU.mult),
                                  reads=[self.bank[b3]] + d["cs"].all(), writes=t2.all())
                            P.run("dve", lambda e, ost=ost, t1=t1, t2=t2: e.tensor_tensor(ost.t[:, :], t1.t[:, :], t2.t[:, :], ALU.add),
                                  reads=t1.all() + t2.all(), writes=ost.all())
                            dst = qd[0][:, c, tsl] if isq else (kad[:, tt, sl] if self.fused else kad[:, tsl])
                            nm = ("qa%d" % l) if isq else (("kvo%d" if self.fused else "kao%d") % l)
                        elif c < 16:
                            isq = (5 <= c < 9) or (10 <= c < 14)
                            sc = 0.125 if isq else 1.0
                            P.run("act", lambda e, ost=ost, bk=bk, sc=sc: e.activation(out=ost.t[:, :], in_=ps[:, bk, :], func=AF.Copy, scale=sc),
                                  reads=[self.bank[bk]], writes=ost.all())
                            if c < 9:
                                dst, nm = qd[1][:, c - 5, tsl], "qb%d" % l
                            elif c == 9:
                                dst, nm = (kbd[:, tt, sl] if self.fused else kbd[:, tsl]), ("kvo%d" if self.fused else "kbo%d") % l
                            elif c < 14:
                                dst, nm = qd[2][:, c - 10, tsl], "qc%d" % l
                            else:
                                dst, nm = (kcd[:, c - 14, tt, sl] if self.fused else kcd[:, c - 14, tsl]), ("kvo%d" if self.fused else "kco%d") % l
                        else:
                            P.run("act", lambda e, ost=ost, bk=bk: e.activation(out=ost.t[:, :], in_=ps[:, bk, :], func=AF.Sigmoid),
                                  reads=[self.bank[bk]], writes=ost.all())
                            dst, nm = gd[:, c - 16, tsl], "gt%d" % l
                        P.dma("sp", dst, ost.t[:, :], reads=ost.all(), writes=[self.dbuf(nm, tt)])
                ws = self.wslot(d)
                P.dma("pool", ws.t[:, 0:4096], wv.rearrange("p k c -> p (k c)"), reads=wv_r, writes=ws.all())
                for tb in range(TT // 128):
                    bk = self.nbank()
                    self.mm_group(bk, 8, lambda k, tb=tb: H.t[:, k, tb * 128:(tb + 1) * 128],
                                  lambda k: ws.t[:, k * 512:(k + 1) * 512],
                                  reads=ws.all() + [H.b(kc, tb // 4) for kc in range(8)])
                    ost = self.ring(d, "ost")
                    P.run("act", lambda e, ost=ost, bk=bk: e.activation(out=ost.t[:, :], in_=ps[:, bk, :], func=AF.Copy),
                          reads=[self.bank[bk]], writes=ost.all())
                    P.dma("sp", vd[t0 + tb * 128:t0 + (tb + 1) * 128, :], ost.t[:, :], reads=ost.all(), writes=[self.dbuf(("kvo%d" if self.fused else "vo%d") % l, tt)])
                if self.fused:
                    self.gather_tile(l, tt)

    def stage_dense2(self, l):
        nc, P, T = self.nc, self.P, self.T
        ps = self.ps
        xres = self.dram("xres%d" % l, [128, 8, T], F32)
        last = self.final and l == max(s[1] for s in self.stages)
        if last:
            xout = self.dram("outT", [128, 8, T], F32)
            xoname = "outT"
        else:
            xout = self.dram("xin%d" % (l + 1), [128, 8, T], F32)
            xoname = "xin%d" % (l + 1)
        wi = self.wdram("wf2i", l)
        wo = self.wdram("wf2o", l)
        wbr, wbr_r = self.wdram("wbr", l)
        wout, wout_r = self.wdram("wo", l)
        yd = self.dram("yT%d" % l, [128, 12, T], BF16)
        gd = self.dram("gt%d" % l, [128, 24, T], BF16)
        vec = self.vec[l]
        with ExitStack() as es:
            d = self.alloc_dense(es, l, with_y=True)
            X, H, Y = d["x"], d["h"], d["y"]
            for tt in range(self.NTT):
                t0 = tt * TT
                P.dma("sp", X.t[:, :, :], xres[:, :, t0:t0 + TT], reads=[self.dbuf("xres%d" % l, tt)], writes=X.all())
                if tt == 0:
                    P.dma("sp", Y.t[:, :, :], yd[:, :, t0:t0 + TT], reads=[self.dbuf("yT%d" % l)], writes=Y.all())
                for oc in range(8):
                    G = d["g"][oc % 2]
                    P.dma("sp", G.t[:, :, :], gd.rearrange("p (n o) t -> p n o t", n=3)[:, :, oc, t0:t0 + TT],
                          reads=[self.dbuf("gt%d" % l, tt)], writes=G.all())
                    ws = self.wslot(d)
                    P.dma("pool", ws.t[:, 0:1536], wbr[oc].rearrange("p k c -> p (k c)"), reads=wbr_r, writes=ws.all())
                    for sb in range(2):
                        sl = slice(sb * 512, (sb + 1) * 512)
                        acc = None
                        for n in range(3):
                            bk = self.nbank()
                            self.mm_group(bk, 4, lambda k, n=n: ws.t[:, (n * 4 + k) * 128:(n * 4 + k + 1) * 128],
                                          lambda k, n=n, sl=sl: Y.t[:, n * 4 + k, sl], reads=ws.all() + Y.all())
                            tmp = self.ring(d, "f1")
                            P.run("dve", lambda e, tmp=tmp, bk=bk, n=n, sl=sl, G=G: e.tensor_tensor(tmp.t[:, :], ps[:, bk, :], G.t[:, n, sl], ALU.mult),
                                  reads=[self.bank[bk]] + G.all(), writes=tmp.all())
                            if n == 0:
                                acc = tmp
                            elif n == 1:
                                P.run("pool", lambda e, acc=acc, tmp=tmp: e.tensor_tensor(acc.t[:, :], acc.t[:, :], tmp.t[:, :], ALU.add),
                                      reads=acc.all() + tmp.all(), writes=acc.all())
                            else:
                                P.run("pool", lambda e, acc=acc, tmp=tmp, oc=oc, sl=sl: e.tensor_tensor(H.t[:, oc, sl], acc.t[:, :], tmp.t[:, :], ALU.add),
                                      reads=acc.all() + tmp.all(), writes=[H.b(oc, sb)])
                if tt + 1 < self.NTT:
                    P.dma("sp", Y.t[:, :, :], yd[:, :, t0 + TT:t0 + 2 * TT], reads=[self.dbuf("yT%d" % l)], writes=Y.all())
                for oc in range(8):
                    ws = self.wslot(d)
                    P.dma("pool", ws.t[:, 0:1024], wout[oc].rearrange("p k c -> p (k c)"), reads=wout_r, writes=ws.all())
                    for sb in range(2):
                        sl = slice(sb * 512, (sb + 1) * 512)
                        bk = self.nbank()
                        self.mm_group(bk, 8, lambda k: ws.t[:, k * 128:(k + 1) * 128], lambda k, sl=sl: H.t[:, k, sl],
                                      reads=ws.all() + [H.b(kc, sb) for kc in range(8)])
                        P.run("dve", lambda e, bk=bk, oc=oc, sl=sl: e.tensor_tensor(X.t[:, oc, sl], ps[:, bk, :], X.t[:, oc, sl], ALU.add),
                              reads=[self.bank[bk], X.b(oc, sb)], writes=[X.b(oc, sb)])
                self.rmsnorm(d, vec[:, 16:24])
                self.ffn(d, wi, wo)
                if last:
                    self.rmsnorm(d, self.gfin, dst_fn=True)
                o = P.dma("sp", xout[:, :, t0:t0 + TT], X.t[:, :, :], reads=X.all(), writes=[self.dbuf(xoname, tt)])
                if last:
                    P.out_dmas.append(o)

    def stage_attn_a(self, l):
        nc, P, T, S = self.nc, self.P, self.T, self.S
        ps = self.ps
        NKT, NQ = self.NKT, self.NQ
        qd = self.dram("qa%d" % l, [128, 4, T], BF16)
        yd = self.dram("yT%d" % l, [128, 12, T], BF16)
        with ExitStack() as es:
            K = Tile(es.enter_context(self.sbt("kA", [128, S], BF16)))
            V = Tile(es.enter_context(self.sbt("vA", [128, 2, NKT, 128], BF16)))
            Q = Tile(es.enter_context(self.sbt("qA", [128, 4, T], BF16)))
            PT = [Tile(es.enter_context(self.sbt("ptA%d" % i, [128, 1024], BF16))) for i in range(3)]
            RC = Tile(es.enter_context(self.sbt("rcA", [64, 512], F32)))
            YS = [Tile(es.enter_context(self.sbt("ysA%d" % i, [64, 512], BF16))) for i in range(2)]
            kb = self.dbuf("kaf%d" % l)
            vb = self.dbuf("vf%d" % l)
            P.dma("sp", Q.t[:, :, :], qd, reads=[self.dbuf("qa%d" % l, tt) for tt in range(self.NTT)], writes=Q.all())
            P.run("pool", lambda e: e.memset(V.t[:, :, :, 64:128], 1.0), writes=V.all())
            NT_ = self.NT
            for j in range(4):
                P.dma("sp", K.t[:, j * T:(j + 1) * T].rearrange("p (t c) -> p t c", c=1024), self.seg_k(l, "a", j), reads=self.kv_rbuf(l), writes=K.all())
                for kv in range(2):
                    P.dma("sp", V.t[:, kv, j * NT_:(j + 1) * NT_, 0:64],
                          self.seg_v(l, j).rearrange("(k p) c -> p k c", p=128)[:, :, kv * 64:(kv + 1) * 64],
                          reads=self.kv_rbuf(l), writes=V.all())
            ysc = [0]

            def unit_a(g, qt):
                        qs = slice(qt * 512, (qt + 1) * 512)
                        ob = [6, 7]

                        def qk(i):
                            sbk = 2 * (i % 3)
                            for hh in range(2):
                                pr = slice(hh * 64, (hh + 1) * 64)
                                P.run("pe", lambda e, sbk=sbk, hh=hh, pr=pr, i=i: e.matmul(
                                    ps[:, sbk + hh, :], K.t[pr, i * 128:(i + 1) * 128], Q.t[pr, g, qs], start=True, stop=True),
                                    reads=(K.all() + Q.all()), writes=[self.bank[sbk + hh]])

                        def ex(i):
                            sbk = 2 * (i % 3)
                            pt = PT[i % 3]
                            P.run("act", lambda e, sbk=sbk, pt=pt: e.activation(
                                out=pt.t[:, :], in_=ps[:, sbk:sbk + 2, :].rearrange("p a b -> p (a b)"), func=AF.Exp),
                                reads=[self.bank[sbk], self.bank[sbk + 1]], writes=pt.all())

                        def pv(i):
                            pt = PT[i % 3]
                            for hh in range(2):
                                P.run("pe", lambda e, pt=pt, hh=hh, i=i: e.matmul(
                                    ps[:, ob[hh], :], V.t[:, hh, i, :], pt.t[:, hh * 512:(hh + 1) * 512], start=(i == 0), stop=(i == NKT - 1)),
                                    reads=(V.all() + pt.all()), writes=[self.bank[ob[hh]]])

                        qk(0)
                        qk(1)
                        for i in range(NKT):
                            ex(i)
                            if i + 2 < NKT:
                                qk(i + 2)
                            pv(i)
                        for hh in range(2):
                            pr = slice(hh * 64, (hh + 1) * 64)
                            P.run("dve", lambda e, hh=hh: e.reciprocal(RC.t[:, :], ps[64:128, ob[hh], :]),
                                  reads=[self.bank[ob[hh]]], writes=RC.all())
                            ys = YS[ysc[0] % 2]
                            ysc[0] += 1
                            P.run("dve", lambda e, hh=hh, ys=ys: e.tensor_tensor(ys.t[:, :], ps[0:64, ob[hh], :], RC.t[:, :], ALU.mult),
                                  reads=[self.bank[ob[hh]]] + RC.all(), writes=ys.all())
                            P.dma("sp", yd[pr, g, qs], ys.t[:, :], reads=ys.all(), writes=[self.dbuf("yT%d" % l)])

            for g in range(4):
                for qt in range(NQ):
                    unit_a(g, qt)

    def stage_attn_c(self, l):
        nc, P, T, S = self.nc, self.P, self.T, self.S
        ps = self.ps
        NKT, NQ, NT = self.NKT, self.NQ, self.NT
        qd = self.dram("qc%d" % l, [128, 4, T], BF16)
        yd = self.dram("yT%d" % l, [128, 12, T], BF16)
        gcd = self.dram("gc", [4, 128, 1152], F32)
        ced = self.dram("cedge", [4, 2, 128, 512], F32)
        cbd = self.dram("cb", [128, 20], F32)
        vec = self.vec[l]
        lam_init = 0.8 - 0.6 * math.exp(-0.3 * l)
        with ExitStack() as es:
            sbt = lambda n, s, dt: es.enter_context(self.sbt(n, s, dt))
            K = Tile(sbt("kC", [128, S], BF16))
            V = Tile(sbt("vC", [128, NKT, 128], BF16))
            Q = Tile(sbt("qC", [128, 2, T], BF16))
            GC = Tile(sbt("gC", [128, 4, 1152], F32))
            CE = Tile(sbt("ceC", [128, 8, 512], F32))
            CB = Tile(sbt("cbC", [128, 20], F32))
            LM = Tile(sbt("lmC", [128, 8], F32))
            LT = Tile(sbt("ltC", [128, 64], F32))
            PT = [Tile(sbt("ptC%d" % i, [128, 1024], BF16)) for i in range(4)]
            SS = [Tile(sbt("ssC%d" % i, [128, 1024], F32)) for i in range(2)]
            RL = [Tile(sbt("rlC%d" % i, [128, 512], F32)) for i in range(2)]
            O1 = Tile(sbt("o1C", [128, 512], F32))
            O2 = Tile(sbt("o2C", [128, 512], F32))
            SQ = Tile(sbt("sqC", [128, 512], BF16))
            RT = Tile(sbt("rtC", [128, 512], F32))
            YS = [Tile(sbt("ysC%d" % i, [128, 512], BF16)) for i in range(2)]
            ACCT = sbt("accC", [128, 1024], F32)
            ACCB = [Buf(), Buf()]
            ONESF = Tile(sbt("onesfC", [128, 128], F32))
            P.run("pool", lambda e: e.memset(ONESF.t[:, :], 1.0), writes=ONESF.all())
            EPSC = Tile(sbt("epsC", [128, 1], F32))
            P.run("pool", lambda e: e.memset(EPSC.t[:, :], EPS), writes=EPSC.all())
            cB = Buf()
            for h in range(4):
                P.dma("sp", GC.t[:, h, :], gcd[h], writes=[cB])
                for e2 in range(2):
                    P.dma("sp", CE.t[:, h * 2 + e2, :], ced[h, e2], writes=[cB])
            P.dma("sp", CB.t[:, :], cbd, writes=[cB])
            for i in range(2):
                P.run("dve", lambda e, i=i: e.tensor_tensor(LT.t[:, :], vec[:, 40 + 128 * i:104 + 128 * i], vec[:, 104 + 128 * i:168 + 128 * i], ALU.mult),
                      reads=[self.cB], writes=LT.all())
                P.run("dve", lambda e, i=i: e.reduce_sum(LM.t[:, i:i + 1], LT.t[:, :], mybir.AxisListType.X), reads=LT.all(), writes=LM.all())
            P.run("act", lambda e: e.activation(out=LM.t[:, 0:2], in_=LM.t[:, 0:2], func=AF.Exp), reads=LM.all(), writes=LM.all())
            P.run("dve", lambda e: e.tensor_tensor(LM.t[:, 2:3], LM.t[:, 1:2], LM.t[:, 0:1], ALU.subtract), reads=LM.all(), writes=LM.all())
            P.run("dve", lambda e: e.tensor_scalar(LM.t[:, 2:3], LM.t[:, 2:3], -lam_init, None, ALU.add), reads=LM.all(), writes=LM.all())
            P.run("dve", lambda e: e.tensor_scalar(LM.t[:, 3:4], vec[:, 26:27], 1.0 - lam_init, None, ALU.mult), reads=LM.all() + [self.cB], writes=LM.all())
            ysc = [0]

            def unit_c(kv, hl, qt):
                        h = 2 * kv + hl
                        qs = slice(qt * 512, (qt + 1) * 512)
                        ob = [4, 5]
                        lb = [6, 7]

                        def kind(i):
                            if i < NT:
                                dl = i - 4 * qt
                                if -1 <= dl <= 4:
                                    off = 512 - 128 * dl
                                    return ("near", GC.t[:, h, off:off + 512])
                                return ("far", h * 5 + (0 if dl < 0 else 1))
                            if i == NT and qt == NQ - 1:
                                return ("near", CE.t[:, h * 2 + 1, :])
                            if i == NKT - 1 and qt == 0:
                                return ("near", CE.t[:, h * 2 + 0, :])
                            return ("far", h * 5 + 1 + i // NT)

                        def qk(i):
                            sbk = 2 * (i % 2)
                            for m in range(2):
                                mr = slice(m * 64, (m + 1) * 64)
                                P.run("pe", lambda e, sbk=sbk, m=m, mr=mr, i=i: e.matmul(
                                    ps[:, sbk + m, :], K.t[mr, i * 128:(i + 1) * 128], Q.t[mr, hl, qs], start=True, stop=True),
                                    reads=(K.all() + Q.all()), writes=[self.bank[sbk + m]])

                        def ex(i):
                            sbk = 2 * (i % 2)
                            pt = PT[i % 4]
                            kd = kind(i)
                            if kd[0] == "far":
                                col = kd[1]
                                P.run("act", lambda e, sbk=sbk, pt=pt, col=col: e.activation(
                                    out=pt.t[:, :], in_=ps[:, sbk:sbk + 2, :].rearrange("p a b -> p (a b)"), func=AF.Exp, bias=CB.t[:, col:col + 1]),
                                    reads=[self.bank[sbk], self.bank[sbk + 1], cB], writes=pt.all())
                            else:
                                bt = kd[1]
                                ss = SS[i % 2]
                                for m in range(2):
                                    P.run("dve", lambda e, sbk=sbk, m=m, ss=ss, bt=bt: e.tensor_tensor(
                                        ss.t[:, m * 512:(m + 1) * 512], ps[:, sbk + m, :], bt, ALU.add),
                                        reads=[self.bank[sbk + m], cB], writes=ss.all())
                                P.run("act", lambda e, ss=ss, pt=pt: e.activation(out=pt.t[:, :], in_=ss.t[:, :], func=AF.Exp),
                                      reads=ss.all(), writes=pt.all())

                        def pv(i):
                            pt = PT[i % 4]
                            for m in range(2):
                                P.run("pe", lambda e, pt=pt, m=m, i=i: e.matmul(
                                    ps[:, ob[m], :], V.t[:, i, :], pt.t[:, m * 512:(m + 1) * 512], start=(i == 0), stop=(i == NKT - 1)),
                                    reads=(V.all() + pt.all()), writes=[self.bank[ob[m]]])
                            if i % 4 == 3:
                                for m in range(2):
                                    P.run("pe", lambda e, pt=pt, m=m, i=i: e.matmul(
                                        ps[:, lb[m], :], self.ones[:, :], pt.t[:, m * 512:(m + 1) * 512], start=(i == 3), stop=False),
                                        reads=[self.cB] + pt.all(), writes=[self.bank[lb[m]]])
                            elif i == 0:
                                P.run("dve", lambda e, pt=pt: e.tensor_copy(ACCT[:, :], pt.t[:, :]), reads=pt.all(), writes=ACCB)
                            else:
                                P.run("dve", lambda e, pt=pt: e.tensor_tensor(ACCT[:, :], ACCT[:, :], pt.t[:, :], ALU.add),
                                      reads=pt.all() + ACCB, writes=ACCB)

                        qk(0)
                        qk(1)
                        for i in range(NKT):
                            ex(i)
                            if i + 2 < NKT:
                                qk(i + 2)
                            pv(i)
                        for m in range(2):
                            P.run("pe", lambda e, m=m: e.matmul(ps[:, lb[m], :], ONESF.t[:, :], ACCT[:, m * 512:(m + 1) * 512], start=False, stop=True),
                                  reads=ONESF.all() + ACCB, writes=[self.bank[lb[m]]])
                        for m in range(2):
                            P.run("act", lambda e, m=m: e.activation(out=RL[m].t[:, :], in_=ps[:, lb[m], :], func=AF.Ln), reads=[self.bank[lb[m]]], writes=RL[m].all())
                            P.run("act", lambda e, m=m: e.activation(out=RL[m].t[:, :], in_=RL[m].t[:, :], func=AF.Exp, scale=-1.0), reads=RL[m].all(), writes=RL[m].all())
                        P.run("dve", lambda e: e.tensor_tensor(O1.t[:, :], ps[:, ob[0], :], RL[0].t[:, :], ALU.mult),
                              reads=[self.bank[ob[0]]] + RL[0].all(), writes=O1.all())
                        P.run("dve", lambda e: e.tensor_tensor(O2.t[:, :], ps[:, ob[1], :], RL[1].t[:, :], ALU.mult),
                              reads=[self.bank[ob[1]]] + RL[1].all(), writes=O2.all())
                        P.run("dve", lambda e: e.scalar_tensor_tensor(out=O1.t[:, :], in0=O2.t[:, :], scalar=LM.t[:, 2:3], in1=O1.t[:, :], op0=ALU.mult, op1=ALU.add),
                              reads=O1.all() + O2.all() + LM.all(), writes=O1.all())
                        P.run("act", lambda e: e.activation(out=SQ.t[:, :], in_=O1.t[:, :], func=AF.Square), reads=O1.all(), writes=SQ.all())
                        b2 = 0
                        self.mm_group(b2, 1, lambda k: self.ones[:, :], lambda k: SQ.t[:, :], reads=[self.cB] + SQ.all())
                        P.run("act", lambda e, b2=b2: e.activation(out=RT.t[:, :], in_=ps[:, b2, :], func=AF.Ln, bias=EPSC.t[:, 0:1], scale=1.0 / 128.0),
                              reads=[self.bank[b2]] + EPSC.all(), writes=RT.all())
                        P.run("act", lambda e: e.activation(out=RT.t[:, :], in_=RT.t[:, :], func=AF.Exp, scale=-0.5), reads=RT.all(), writes=RT.all())
                        ys = YS[ysc[0] % 2]
                        ysc[0] += 1
                        P.run("dve", lambda e, ys=ys: e.scalar_tensor_tensor(out=ys.t[:, :], in0=O1.t[:, :], scalar=LM.t[:, 3:4], in1=RT.t[:, :], op0=ALU.mult, op1=ALU.mult),
                              reads=O1.all() + RT.all() + LM.all(), writes=ys.all())
                        P.dma("sp", yd[:, 8 + h, qs], ys.t[:, :], reads=ys.all(), writes=[self.dbuf("yT%d" % l)])

            for kv in range(2):
                for j in range(4):
                    P.dma("sp", K.t[:, j * T:(j + 1) * T].rearrange("p (t c) -> p t c", c=1024), self.seg_k(l, "c", j, kv), reads=self.kv_rbuf(l), writes=K.all())
                    P.dma("sp", V.t[:, j * NT:(j + 1) * NT, :],
                          self.seg_v(l, j).rearrange("(k p) c -> p k c", p=128)[:, :, 256 + kv * 128:256 + (kv + 1) * 128],
                          reads=self.kv_rbuf(l), writes=V.all())
                P.dma("sp", Q.t[:, :, :], qd[:, 2 * kv:2 * kv + 2, :], reads=[self.dbuf("qc%d" % l, tt) for tt in range(self.NTT)], writes=Q.all())
                for hl in range(2):
                    for qt in range(NQ):
                        unit_c(kv, hl, qt)

    def stage_attn_b(self, l):
        nc, P, T, S = self.nc, self.P, self.T, self.S
        ps = self.ps
        NKT, NT = self.NKT, self.NT
        qd = self.dram("qb%d" % l, [128, 4, T], BF16)
        yd = self.dram("yT%d" % l, [128, 12, T], BF16)
        bbd = self.dram("bb", [10, 128, 512], F32)
        vec = self.vec[l]
        with ExitStack() as es:
            sbt = lambda n, s, dt: es.enter_context(self.sbt(n, s, dt))
            K = Tile(sbt("kB", [128, NT + 2, 128], BF16))
            V = Tile(sbt("vB", [128, 2, NT + 2, 128], BF16))
            Q = Tile(sbt("qB", [128, 4, T], BF16))
            BB = Tile(sbt("bbB", [128, 10, 512], F32))
            ES = Tile(sbt("esB", [128, 8], F32))
            SS = [Tile(sbt("ssB%d" % i, [128, 512], F32)) for i in range(3)]
            PT = [Tile(sbt("ptB%d" % i, [128, 512], BF16)) for i in range(3)]
            DEN = Tile(sbt("denB", [128, 512], F32))
            RC = Tile(sbt("rcB", [64, 512], F32))
            YS = [Tile(sbt("ysB%d" % i, [64, 512], BF16)) for i in range(2)]
            cB = Buf()
            kb = self.dbuf("kbf%d" % l)
            vb = self.dbuf("vf%d" % l)
            P.run("pool", lambda e: e.memset(V.t[:, :, :, 64:128], 1.0), writes=V.all())
            kr = self.kv_rbuf(l)
            k3, k0, k1 = self.seg_k(l, "b", 3), self.seg_k(l, "b", 0), self.seg_k(l, "b", 1)
            v3, v0, v1 = self.seg_v(l, 3), self.seg_v(l, 0), self.seg_v(l, 1)
            P.dma("sp", K.t[:, 0, :], k3[:, self.NTT - 1, 896:1024], reads=kr, writes=K.all())
            P.dma("sp", K.t[:, 1:NT + 1, :].rearrange("p (t k) c -> p t k c", k=8), k0.rearrange("p t (k c) -> p t k c", c=128), reads=kr, writes=K.all())
            P.dma("sp", K.t[:, NT + 1, :], k1[:, 0, 0:128], reads=kr, writes=K.all())
            for kv in range(2):
                cs_ = slice(128 + kv * 64, 128 + (kv + 1) * 64)
                P.dma("sp", V.t[:, kv, 0, 0:64], v3[(NT - 1) * 128:NT * 128, cs_], reads=kr, writes=V.all())
                P.dma("sp", V.t[:, kv, 1:NT + 1, 0:64], v0.rearrange("(k p) c -> p k c", p=128)[:, :, cs_], reads=kr, writes=V.all())
                P.dma("sp", V.t[:, kv, NT + 1, 0:64], v1[0:128, cs_], reads=kr, writes=V.all())
            P.dma("sp", Q.t[:, :, :], qd, reads=[self.dbuf("qb%d" % l, tt) for tt in range(self.NTT)], writes=Q.all())
            for i in range(10):
                P.dma("sp", BB.t[:, i, :], bbd[i], writes=[cB])
            BH = Tile(sbt("bhB", [128, 10, 512], BF16))
            BL = Tile(sbt("blB", [128, 10, 512], BF16))
            BT = Tile(sbt("btB", [128, 512], F32))
            for i in range(10):
                P.run("dve", lambda e, i=i: e.tensor_copy(BH.t[:, i, :], BB.t[:, i, :]), reads=[cB], writes=BH.all())
                P.run("dve", lambda e, i=i: e.tensor_tensor(BT.t[:, :], BB.t[:, i, :], BH.t[:, i, :], ALU.subtract), reads=[cB] + BH.all(), writes=BT.all())
                P.run("dve", lambda e, i=i: e.tensor_copy(BL.t[:, i, :], BT.t[:, :]), reads=BT.all(), writes=BL.all())
            P.run("act", lambda e: e.activation(out=ES.t[:, :], in_=vec[:, 32:40], func=AF.Exp), reads=[self.cB], writes=[cB])
            ysc = [0]

            def unit_b(kv, n):
                    pr = slice(kv * 64, (kv + 1) * 64)
                    ob = 7
                    for dd in range(3):
                        bk = dd * 2
                        var = dd
                        if n == 0 and dd == 0:
                            var = 3
                        if n == NT - 1 and dd == 2:
                            var = 4
                        P.run("pe", lambda e, bk=bk, dd=dd, n=n: e.matmul(
                            ps[:, bk, :], K.t[pr, n + dd, :], Q.t[pr, :, n * 128:(n + 1) * 128], start=True, stop=False),
                            reads=K.all() + Q.all(), writes=[self.bank[bk]])
                        P.run("pe", lambda e, bk=bk, var=var: e.matmul(ps[:, bk, :], self.ident[:, :], BH.t[:, kv * 5 + var, :], start=False, stop=False),
                              reads=[self.cB] + BH.all(), writes=[self.bank[bk]])
                        P.run("pe", lambda e, bk=bk, var=var: e.matmul(ps[:, bk, :], self.ident[:, :], BL.t[:, kv * 5 + var, :], start=False, stop=True),
                              reads=[self.cB] + BL.all(), writes=[self.bank[bk]])
                        pt = PT[dd]
                        P.run("act", lambda e, bk=bk, pt=pt: e.activation(out=pt.t[:, :], in_=ps[:, bk, :], func=AF.Exp), reads=[self.bank[bk]], writes=pt.all())
                    for dd in range(3):
                        pt = PT[dd]
                        P.run("pe", lambda e, pt=pt, dd=dd, n=n: e.matmul(ps[:, ob, :], V.t[:, kv, n + dd, :], pt.t[:, :], start=(dd == 0), stop=(dd == 2)),
                              reads=V.all() + pt.all(), writes=[self.bank[ob]])
                    for g in range(4):
                        gs = slice(g * 128, (g + 1) * 128)
                        P.run("dve", lambda e, g=g, gs=gs: e.tensor_scalar(DEN.t[64:128, gs], ps[64:128, ob, gs], ES.t[64:128, kv * 4 + g:kv * 4 + g + 1], None, ALU.add),
                              reads=[self.bank[ob], cB], writes=DEN.all())
                    P.run("dve", lambda e: e.reciprocal(RC.t[:, :], DEN.t[64:128, :]), reads=DEN.all(), writes=RC.all())
                    ys = YS[ysc[0] % 2]
                    ysc[0] += 1
                    P.run("dve", lambda e, ys=ys: e.tensor_tensor(ys.t[:, :], ps[0:64, ob, :], RC.t[:, :], ALU.mult),
                          reads=[self.bank[ob]] + RC.all(), writes=ys.all())
                    P.dma("sp", yd[pr, 4:8, n * 128:(n + 1) * 128], ys.t[:, :].rearrange("p (g q) -> p g q", g=4), reads=ys.all(), writes=[self.dbuf("yT%d" % l)])

            for kv in range(2):
                for n in range(NT):
                    unit_b(kv, n)


def _t5_bucket_np(rel):
    import jax
    import jax.numpy as jnp
    with jax.default_device(jax.devices("cpu")[0]):
        rel = jnp.asarray(rel, dtype=jnp.int32)
        nb = 16
        max_exact = 8
        side = jnp.where(rel > 0, nb, 0)
        n = jnp.abs(rel)
        nf = jnp.maximum(n, 1).astype(jnp.float32)
        large = max_exact + (jnp.log(nf / max_exact) / math.log(128 / max_exact) * (nb - max_exact)).astype(jnp.int32)
        large = jnp.minimum(large, nb - 1)
        return np.asarray(side + jnp.where(n < max_exact, n, large))


def _rope_tables(S):
    rows = S // 64
    row_ids = np.repeat(np.arange(rows), 64).astype(np.float32)
    col_ids = np.tile(np.arange(64), rows).astype(np.float32)
    freqs = (np.float32(10000.0) ** (-np.arange(0, 32, 2, dtype=np.float32) / np.float32(32))).astype(np.float32)
    ang_r = row_ids[:, None] * freqs
    ang_c = col_ids[:, None] * freqs
    cos = np.concatenate([np.cos(ang_r), np.cos(ang_r), np.cos(ang_c), np.cos(ang_c)], axis=1)
    sin = np.concatenate([np.sin(ang_r), np.sin(ang_r), np.sin(ang_c), np.sin(ang_c)], axis=1)
    return cos.astype(np.float32), sin.astype(np.float32)


def _rmat():
    R = np.zeros((64, 64), np.float32)
    for i in range(64):
        if (i % 32) < 16:
            R[i, i + 16] = -1.0
        else:
            R[i, i - 16] = 1.0
    full = np.zeros((128, 128), np.float32)
    full[0:64, 0:64] = R
    full[64:128, 64:128] = R
    return np.ascontiguousarray(full.T)


def _fm_w(w, ncols_chunks):
    K, C = w.shape
    return np.ascontiguousarray(w.reshape(K // 128, 128, C // 128, 128).transpose(2, 1, 0, 3))


def prep_layer_weights(inp, l):
    out = {}
    for i, nm in ((1, "w_ffn1"), (2, "w_ffn2")):
        wi = inp[nm + "_in"][l]
        g = wi[:, :DFF].reshape(8, 128, NFF, 128)
        u = wi[:, DFF:].reshape(8, 128, NFF, 128)
        gu = np.concatenate([g, u], axis=3)
        out["wf%di%d" % (i, l)] = np.ascontiguousarray(gu.transpose(2, 1, 0, 3))
        wo = inp[nm + "_out"][l]
        out["wf%do%d" % (i, l)] = np.ascontiguousarray(wo.reshape(NFF, 128, 8, 128).transpose(2, 1, 0, 3))
    w = inp["w_in"][l]
    cols = []
    pair = lambda base: [np.r_[base + g * 64:base + g * 64 + 64, base + (g + 4) * 64:base + (g + 4) * 64 + 64] for g in range(4)]
    cols += pair(0)
    cols += [np.arange(512, 640)]
    cols += pair(768)
    cols += [np.arange(1280, 1408)]
    cols += [np.arange(1536 + h * 128, 1536 + (h + 1) * 128) for h in range(4)]
    cols += [np.arange(2048 + k * 128, 2048 + (k + 1) * 128) for k in range(2)]
    cols += [np.arange(2560 + c * 128, 2560 + (c + 1) * 128) for c in range(24)]
    wfm = np.stack([w[:, c] for c in cols], axis=0)
    out["wfm%d" % l] = np.ascontiguousarray(wfm.reshape(40, 8, 128, 128).transpose(0, 2, 1, 3))
    vcols = np.r_[640:768, 1408:1536, 2304:2560]
    out["wv%d" % l] = np.ascontiguousarray(w[:, vcols].reshape(8, 128, 512).transpose(1, 0, 2))
    wb = inp["w_branch"][l]
    prow = np.concatenate([np.r_[g * 64:g * 64 + 64, (g + 4) * 64:(g + 4) * 64 + 64] for g in range(4)])
    wb2 = np.stack([wb[0][prow], wb[1][prow], wb[2]], axis=0)
    out["wbr%d" % l] = np.ascontiguousarray(wb2.reshape(3, 4, 128, 8, 128).transpose(3, 2, 0, 1, 4).reshape(8, 128, 12, 128))
    wo = inp["w_out"][l]
    out["wo%d" % l] = np.ascontiguousarray(wo.reshape(8, 128, 8, 128).transpose(2, 1, 0, 3))
    vec = np.zeros((128, 320), np.float32)
    vec[:, 0:8] = inp["norm_ffn1"][l].reshape(8, 128).T
    vec[:, 8:16] = inp["norm_mix"][l].reshape(8, 128).T
    vec[:, 16:24] = inp["norm_ffn2"][l].reshape(8, 128).T
    vec[:, 24] = np.tile(inp["qnorm_a"][l], 2)
    vec[:, 25] = np.tile(inp["knorm_a"][l], 2)
    vec[:, 26] = inp["subln_c"][l]
    vec[:, 32:40] = inp["sink_b"][l][None, :]
    vec[:, 40:104] = inp["lam_q1"][l][None, :]
    vec[:, 104:168] = inp["lam_k1"][l][None, :]
    vec[:, 168:232] = inp["lam_q2"][l][None, :]
    vec[:, 232:296] = inp["lam_k2"][l][None, :]
    out["vec%d" % l] = vec
    return out


def prep_bias(rel_bias, T):
    NT = T // 128
    k = np.arange(128)[:, None]
    u = np.arange(1152)[None, :] - 512
    bc = _t5_bucket_np(k - u)
    tab_c = rel_bias[:, 8:12]
    gc = np.ascontiguousarray(np.stack([tab_c[:, h][bc] for h in range(4)], axis=0)).astype(np.float32)
    left = tab_c[15]
    right = tab_c[31]
    per_rank = []
    q = np.arange(128)[None, :]
    tab_b = rel_bias[:, 0:8]
    btiles = np.zeros((2, 3, 128, 4, 128), np.float32)
    for dl in (-1, 0, 1):
        rel = 128 * dl + k - q
        bk = _t5_bucket_np(rel)
        ok = np.abs(rel) <= 128
        for kv in range(2):
            for g in range(4):
                v = tab_b[:, kv * 4 + g][bk]
                btiles[kv, dl + 1, :, g, :] = np.where(ok, v, np.float32(NEG))
    for r in range(4):
        ce = np.zeros((4, 2, 128, 512), np.float32)
        cb = np.zeros((128, 20), np.float32)
        for h in range(4):
            ce[h, 0] = gc[h][:, 512 + 128:512 + 128 + 512] if r > 0 else right[h]
            ce[h, 1] = gc[h][:, 0:512] if r < 3 else left[h]
            cb[:, h * 5 + 0] = left[h]
            cb[:, h * 5 + 1] = right[h]
            for s in range(1, 4):
                cb[:, h * 5 + 1 + s] = right[h] if ((r + s) % 4) > r else left[h]
        bb = np.zeros((2, 5, 128, 512), np.float32)
        for kv in range(2):
            for v in range(3):
                bb[kv, v] = btiles[kv, v].reshape(128, 512)
            bb[kv, 3] = bb[kv, 0] if r > 0 else np.float32(NEG)
            bb[kv, 4] = bb[kv, 2] if r < 3 else np.float32(NEG)
        per_rank.append({"cedge": ce, "cb": cb, "bb": np.ascontiguousarray(bb.reshape(10, 128, 512))})
    return gc, per_rank


def rotate_gather(shards, b, r, axis):
    return np.concatenate([shards[b * 4 + (r + j) % 4] for j in range(4)], axis=axis)


_NC_CACHE = {}
DEBUG = None
FUSED = True
PRECONVERT = False


def run_model(inp, T):
    S = 4 * T
    x = np.asarray(inp["x"], np.float32)
    inp = {k: np.asarray(v, np.float32) for k, v in inp.items()}
    assert x.shape == (2, S, D)
    cos, sin = _rope_tables(S)
    gc, per_rank = prep_bias(inp["rel_bias"], T)
    rmat = _rmat()
    lw = {}
    for l in range(2):
        lw.update(prep_layer_weights(inp, l))
    gfin = np.ascontiguousarray(inp["norm_final"].reshape(8, 128).T)

    core_static = []
    for c in range(NCORES):
        b, r = c // 4, c % 4
        sl = slice(r * T, (r + 1) * T)
        cs = np.stack([np.tile(cos[sl].T, (2, 1)), np.tile(sin[sl].T, (2, 1))], axis=1)
        dct = {"cs": np.ascontiguousarray(cs), "gc": gc, "rmat": rmat, "ident": np.eye(128, dtype=np.float32)}
        dct.update(per_rank[r])
        core_static.append(dct)

    def wnames(l, d1, d2):
        n = []
        if d1:
            n += ["wf1i%d" % l, "wf1o%d" % l, "wfm%d" % l, "wv%d" % l]
        if d2:
            n += ["wf2i%d" % l, "wf2o%d" % l, "wbr%d" % l, "wo%d" % l]
        return n

    def wnames_all():
        return wnames(0, True, True) + wnames(1, True, True)

    def launch(key, stages, ext_in, ext_out, final, maps):
        ck = (key, T)
        if ck not in _NC_CACHE:
            _NC_CACHE[ck] = Builder(T, stages, ext_in, ext_out, final).build()
        nc = _NC_CACHE[ck]
        res = run_bass_kernel_spmd(nc, maps, core_ids=list(range(NCORES)))
        if DEBUG is not None:
            DEBUG[key] = res.results
            DEBUG[key + "_in"] = maps
        return res.results

    if FUSED:
        ext_in = ["xin0", "cs", "rmat", "ident", "vec0", "vec1", "gfin", "gc", "cedge", "cb", "bb"] + wnames_all()
        maps = []
        for c in range(NCORES):
            b, r = c // 4, c % 4
            xT = np.ascontiguousarray(x[b, r * T:(r + 1) * T, :].T.reshape(8, 128, T).transpose(1, 0, 2))
            m = {"xin0": xT, "vec0": lw["vec0"], "vec1": lw["vec1"], "gfin": gfin}
            for n in ("cs", "gc", "cedge", "cb", "bb", "rmat", "ident"):
                m[n] = core_static[c][n]
            for n in wnames_all():
                m[n] = lw[n]
            maps.append(m)
        stages = [("d1", 0), ("ag", 0), ("attn", 0), ("d2", 0), ("d1", 1), ("ag", 1), ("attn", 1), ("d2", 1)]
        ck = ("F", T)
        if ck not in _NC_CACHE:
            _NC_CACHE[ck] = Builder(T, stages, ext_in, ["outT"], True, fused=True).build()
        res = run_bass_kernel_spmd(_NC_CACHE[ck], maps, core_ids=list(range(NCORES))).results
        out = np.zeros((2, S, D), np.float32)
        for c in range(NCORES):
            b, r = c // 4, c % 4
            oT = np.asarray(res[c]["outT"], np.float32)
            out[b, r * T:(r + 1) * T, :] = oT.transpose(1, 0, 2).reshape(D, T).T
        return out

    hand = ["xres", "qa", "qb", "qc", "gt"]
    own = ["kao", "kbo", "kco", "vo"]

    ext_in = ["xin0", "cs", "rmat", "vec0"] + wnames(0, True, False)
    ext_out = [h + "0" for h in hand + own]
    maps = []
    for c in range(NCORES):
        b, r = c // 4, c % 4
        xT = np.ascontiguousarray(x[b, r * T:(r + 1) * T, :].T.reshape(8, 128, T).transpose(1, 0, 2))
        m = {"xin0": xT, "cs": core_static[c]["cs"], "rmat": rmat, "vec0": lw["vec0"]}
        for n in wnames(0, True, False):
            m[n] = lw[n]
        maps.append(m)
    res = launch("L1", [("d1", 0)], ext_in, ext_out, False, maps)

    def attn_inputs(res, l):
        mlist = []
        for c in range(NCORES):
            b, r = c // 4, c % 4
            m = {}
            for h in hand:
                m[h + "%d" % l] = res[c][h + "%d" % l]
            m["kaf%d" % l] = rotate_gather([rr["kao%d" % l] for rr in res], b, r, 1)
            m["kbf%d" % l] = rotate_gather([rr["kbo%d" % l] for rr in res], b, r, 1)
            m["kcf%d" % l] = rotate_gather([rr["kco%d" % l] for rr in res], b, r, 2)
            m["vf%d" % l] = rotate_gather([rr["vo%d" % l] for rr in res], b, r, 0)
            for n in ("gc", "cedge", "cb", "bb", "rmat"):
                m[n] = core_static[c][n]
            mlist.append(m)
        return mlist

    maps = attn_inputs(res, 0)
    ext_in = list(maps[0].keys()) + ["vec0", "vec1", "cs"] + wnames(0, False, True) + wnames(1, True, False)
    for c in range(NCORES):
        maps[c]["vec0"] = lw["vec0"]
        maps[c]["vec1"] = lw["vec1"]
        maps[c]["cs"] = core_static[c]["cs"]
        for n in wnames(0, False, True) + wnames(1, True, False):
            maps[c][n] = lw[n]
    ext_out = [h + "1" for h in hand + own]
    if DEBUG is not None:
        ext_out += ["yT0", "xin1"]
    res = launch("L2", [("attn", 0), ("d2", 0), ("d1", 1)], ext_in, ext_out, False, maps)

    maps = attn_inputs(res, 1)
    ext_in = list(maps[0].keys()) + ["vec1", "gfin"] + wnames(1, False, True)
    for c in range(NCORES):
        maps[c]["vec1"] = lw["vec1"]
        maps[c]["gfin"] = gfin
        for n in wnames(1, False, True):
            maps[c][n] = lw[n]
    res = launch("L3", [("attn", 1), ("d2", 1)], ext_in, ["outT"], True, maps)

    out = np.zeros((2, S, D), np.float32)
    for c in range(NCORES):
        b, r = c // 4, c % 4
        oT = np.asarray(res[c]["outT"], np.float32)
        out[b, r * T:(r + 1) * T, :] = oT.transpose(1, 0, 2).reshape(D, T).T
    return out


def kernel(**inputs):
    return run_model(inputs, 4096)
```

```python
import math
from contextlib import ExitStack

import numpy as np
import ml_dtypes

import concourse.bass as bass
import concourse.mybir as mybir
from concourse.bass_utils import run_bass_kernel_spmd

F32 = mybir.dt.float32
BF16 = mybir.dt.bfloat16
ALU = mybir.AluOpType
AF = mybir.ActivationFunctionType
NPBF = ml_dtypes.bfloat16

D = 1024
DFF = 2816
NFF = 22
EPS = 1e-6
NEG = -30000.0
NCORES = 8
TT = 1024


class Buf:
    __slots__ = ("w", "r")

    def __init__(self):
        self.w = []
        self.r = []


class Op:
    __slots__ = ("eng", "fn", "deps", "sem", "val", "signal", "is_dma")


class Prog:
    ENGS = ("pe", "act", "dve", "pool", "sp")
    NDSEM = 12

    def __init__(self, nc, es):
        self.nc = nc
        self.ops = {e: [] for e in self.ENGS}
        self.csem = {e: es.enter_context(nc.semaphore("cs_" + e)) for e in self.ENGS}
        self.dsem = {}
        self.dcount = {}
        self.dhist = {}
        for e in ("pool", "sp", "conv"):
            self.dsem[e] = [es.enter_context(nc.semaphore("ds_%s%d" % (e, i))) for i in range(self.NDSEM)]
            self.dcount[e] = 0
            self.dhist[e] = []
        self.out_dmas = []
        self.ccsem = es.enter_context(nc.semaphore("cc_sem"))
        self.cccount = 0
        self.cchist = []
        self.fence = []
        self.need_fence = {e: False for e in self.ENGS}

    def barrier(self):
        f = []
        for e in self.ENGS:
            comp = [o for o in self.ops[e] if not o.is_dma]
            if comp:
                f.append(comp[-1])
        for e in ("pool", "sp", "conv"):
            f.extend(self.dhist[e][-self.NDSEM:])
        f.extend(self.cchist[-1:])
        self.fence = f
        for e in self.ENGS:
            self.need_fence[e] = True

    def _mk(self, eng, fn, reads, writes, deps):
        o = Op()
        o.eng = eng
        o.fn = fn
        o.sem = None
        o.val = None
        o.signal = False
        o.is_dma = False
        dl = list(deps)
        if self.need_fence[eng]:
            dl.extend(self.fence)
            self.need_fence[eng] = False
        for b in reads:
            dl.extend(b.w)
        for b in writes:
            dl.extend(b.r)
            dl.extend(b.w)
        o.deps = dl
        self.ops[eng].append(o)
        return o

    def _upd(self, o, reads, writes):
        for b in writes:
            b.w = [o]
            b.r = []
        for b in reads:
            if o.is_dma:
                b.r.append(o)
            else:
                b.r = [x for x in b.r if x.is_dma or x.eng != o.eng]
                b.r.append(o)

    def run(self, eng, fn, reads=(), writes=(), deps=()):
        o = self._mk(eng, fn, reads, writes, deps)
        self._upd(o, reads, writes)
        return o

    def dma(self, eng, out, in_, reads=(), writes=(), deps=(), ring=None):
        o = self._mk(eng, lambda e: e.dma_start(out=out, in_=in_), reads, writes, deps)
        o.is_dma = True
        rk = eng if ring is None else ring
        k = self.dcount[rk]
        self.dcount[rk] = k + 1
        o.sem = self.dsem[rk][k % self.NDSEM]
        o.val = 16 * (k // self.NDSEM + 1)
        if k >= self.NDSEM:
            o.deps.append(self.dhist[rk][k - self.NDSEM])
        self.dhist[rk].append(o)
        self._upd(o, reads, writes)
        return o

    def coll(self, fn, reads=(), writes=()):
        o = self._mk("pool", fn, reads, writes, ())
        o.is_dma = True
        o.signal = "cc"
        self.cccount += 1
        o.sem = self.ccsem
        o.val = self.cccount
        self.cchist.append(o)
        self._upd(o, reads, writes)
        return o

    def replay(self, block):
        for e in self.ENGS:
            for o in self.ops[e]:
                for d in o.deps:
                    if not d.is_dma:
                        if d.eng == "pe" and o.eng == "pe":
                            continue
                        d.signal = True
        for e in self.ENGS:
            c = 0
            for o in self.ops[e]:
                if (not o.is_dma) and o.signal:
                    c += 1
                    o.sem = self.csem[e]
                    o.val = c

        finals = list(self.out_dmas)
        for e in ("pool", "sp", "conv"):
            finals.extend(self.dhist[e][-self.NDSEM:])

        def go(ename, eng, tail=False):
            waited = {}
            for o in self.ops[ename]:
                for d in o.deps:
                    if (not d.is_dma) and d.eng == "pe" and ename == "pe":
                        continue
                    key = id(d.sem)
                    if waited.get(key, 0) >= d.val:
                        continue
                    eng.wait_ge(d.sem, d.val)
                    waited[key] = d.val
                ins = o.fn(eng)
                if o.signal == "cc":
                    ins.then_inc(o.sem)
                elif o.is_dma:
                    ins.then_inc(o.sem, 16)
                elif o.signal:
                    ins.then_inc(o.sem, 1)
            if tail:
                for d in finals:
                    key = id(d.sem)
                    if waited.get(key, 0) >= d.val:
                        continue
                    eng.wait_ge(d.sem, d.val)
                    waited[key] = d.val

        @block.tensor
        def _(t):
            go("pe", t)

        @block.scalar
        def _(s):
            go("act", s)

        @block.vector
        def _(v):
            go("dve", v)

        @block.gpsimd
        def _(g):
            go("pool", g)

        @block.sync
        def _(s):
            go("sp", s, tail=True)


class Tile:
    def __init__(self, t, *dims):
        self.t = t
        self.dims = dims
        n = 1
        for d in dims:
            n *= d
        self.bufs = [Buf() for _ in range(max(n, 1))]

    def b(self, *idx):
        k = 0
        for i, d in zip(idx, self.dims):
            k = k * d + i
        return self.bufs[k]

    def all(self):
        return self.bufs


class Builder:
    def __init__(self, T, stages, ext_in, ext_out, last_layer_final, fused=False):
        self.fused = fused
        self.T = T
        self.S = 4 * T
        self.NT = T // 128
        self.NKT = self.S // 128
        self.NQ = T // 512
        self.NTT = T // TT
        self.stages = stages
        self.ext_in = set(ext_in)
        self.ext_out = set(ext_out)
        self.final = last_layer_final
        self.nc = bass.Bass("TRN2", target_bir_lowering=False, num_devices=NCORES)
        self.dr = {}
        self.drb = {}

    WSHAPES = {"wf1i": [NFF, 128, 8, 256], "wf1o": [8, 128, NFF, 128], "wfm": [40, 128, 8, 128], "wv": [1, 128, 8, 512],
               "wbr": [8, 128, 12, 128], "wo": [8, 128, 8, 128], "wf2i": [NFF, 128, 8, 256], "wf2o": [8, 128, NFF, 128]}

    def emit_conv(self, l):
        for base in ("wf1i", "wf1o", "wfm", "wv", "wbr", "wo", "wf2i", "wf2o"):
            shp = self.WSHAPES[base]
            nm = "%s%d" % (base, l)
            src = self.dram(nm, shp if base != "wv" else [128, 8, 512], F32)
            dst = self.dram(nm + "b", shp if base != "wv" else [128, 8, 512], BF16)
            if base == "wv":
                s2, d2 = src.rearrange("p k c -> p (k c)"), dst.rearrange("p k c -> p (k c)")
            else:
                s2, d2 = src.rearrange("j p k c -> (j p) (k c)"), dst.rearrange("j p k c -> (j p) (k c)")
            self.P.dma("pool", d2, s2, writes=[self.dbuf(nm + "b")], ring="conv")

    def wdram(self, base, l):
        shp = self.WSHAPES[base]
        if base == "wv":
            shp = [128, 8, 512]
        nm = "%s%d" % (base, l)
        if self.fused and PRECONVERT:
            return self.dram(nm + "b", shp, BF16), [self.dbuf(nm + "b")]
        return self.dram(nm, shp, F32), []

    def sbt(self, name, shape, dt):
        self._uid = getattr(self, "_uid", 0) + 1
        return self.nc.sbuf_tensor("%s_u%d" % (name, self._uid), shape, dt)

    def dram(self, name, shape, dt):
        if name in self.dr:
            return self.dr[name]
        if name in self.ext_in:
            kind = "ExternalInput"
        elif name in self.ext_out:
            kind = "ExternalOutput"
        else:
            kind = "Internal"
        t = self.nc.dram_tensor(name, list(shape), dt, kind=kind).ap()
        self.dr[name] = t
        return t

    def dbuf(self, name, *idx):
        key = (name,) + idx
        if key not in self.drb:
            self.drb[key] = Buf()
        return self.drb[key]

    def build(self):
        nc = self.nc
        T, S = self.T, self.S
        with ExitStack() as es:
            self.P = P = Prog(nc, es)
            self.ps = es.enter_context(nc.psum_tensor("ps", [128, 8, 512], F32))
            self.bank = [Buf() for _ in range(8)]
            self.bank_rr = 0
            self.ones = es.enter_context(self.sbt("ones", [128, 128], BF16))
            self.blk = es.enter_context(self.sbt("blk", [128, 128], BF16))
            self.rmat = es.enter_context(self.sbt("sb_rmat", [128, 128], BF16))
            self.cB = Buf()
            P.run("pool", lambda e: e.memset(self.ones[:], 1.0), writes=[self.cB])
            P.run("pool", lambda e: e.memset(self.blk[:], 0.0), writes=[self.cB])
            P.run("pool", lambda e: e.memset(self.blk[0:64, 0:64], 1.0), writes=[self.cB])
            P.run("pool", lambda e: e.memset(self.blk[64:128, 64:128], 1.0), writes=[self.cB])
            rm = self.dram("rmat", [128, 128], F32)
            P.dma("pool", self.rmat[:], rm, writes=[self.cB])
            self.ident = es.enter_context(self.sbt("sb_ident", [128, 128], BF16))
            P.dma("pool", self.ident[:], self.dram("ident", [128, 128], F32), writes=[self.cB])
            self.vec = {}
            for l in sorted(set(s[1] for s in self.stages)):
                vt = es.enter_context(self.sbt("sb_vec%d" % l, [128, 320], F32))
                vd = self.dram("vec%d" % l, [128, 320], F32)
                P.dma("sp", vt[:], vd, writes=[self.cB])
                self.vec[l] = vt
            if self.final:
                self.gfin = es.enter_context(self.sbt("sb_gfin", [128, 8], F32))
                P.dma("sp", self.gfin[:], self.dram("gfin", [128, 8], F32), writes=[self.cB])
            if self.fused and PRECONVERT:
                self.emit_conv(0)
            for st in self.stages:
                kind, l = st
                if self.fused and PRECONVERT and kind == "attn" and l == 0:
                    self.emit_conv(1)
                if kind == "d1":
                    P.barrier()
                    self.stage_dense1(l)
                elif kind == "attn":
                    P.barrier()
                    self.stage_attn_a(l)
                    P.barrier()
                    self.stage_attn_c(l)
                    P.barrier()
                    self.stage_attn_b(l)
                elif kind == "d2":
                    P.barrier()
                    self.stage_dense2(l)
                elif kind == "ag":
                    self.stage_gather(l)
            block = es.enter_context(nc.Block())
            P.replay(block)
        return nc

    def seg_k(self, l, which, j, kv=0):
        T = self.T
        if self.fused:
            rb = self.dram("rb%d" % l, [4, 2 * self.NTT, 512, 1024], BF16)
            r0 = {"a": 0, "b": 128, "c": 256 + kv * 128}[which]
            return rb[j, 0:self.NTT].rearrange("t r c -> r t c")[r0:r0 + 128]
        if which == "a":
            a = self.dram("kaf%d" % l, [128, self.S], BF16)[:, j * T:(j + 1) * T]
        elif which == "b":
            a = self.dram("kbf%d" % l, [128, self.S], BF16)[:, j * T:(j + 1) * T]
        else:
            a = self.dram("kcf%d" % l, [128, 2, self.S], BF16)[:, kv, j * T:(j + 1) * T]
        return a.rearrange("p (t c) -> p t c", c=1024)

    def seg_v(self, l, j):
        T = self.T
        if self.fused:
            rb = self.dram("rb%d" % l, [4, 2 * self.NTT, 512, 1024], BF16)
            return rb[j, self.NTT:2 * self.NTT].rearrange("t r (a c) -> (t r a) c", c=512)
        return self.dram("vf%d" % l, [self.S, 512], BF16)[j * T:(j + 1) * T, :]

    def kv_rbuf(self, l):
        if self.fused:
            return [self.dbuf("rb%d" % l)]
        return [self.dbuf("kvext%d" % l)]

    def gather_tile(self, l, tt):
        P, NTT = self.P, self.NTT
        rg = [[0, 1, 2, 3], [4, 5, 6, 7]]
        own = self.dram("kvo%d" % l, [2 * NTT, 512, 1024], BF16)
        gat = self.dram("kvg%d" % l, [2 * NTT, 2048, 1024], BF16)
        for n in (tt, NTT + tt):
            P.coll(lambda e, n=n: e.collective_compute("AllGather", ALU.bypass, replica_groups=rg,
                                                       ins=[own[n].opt()], outs=[gat[n].opt()]),
                   reads=[self.dbuf("kvo%d" % l, tt)], writes=[self.dbuf("kvg%d" % l, n)])

    def stage_gather(self, l):
        P, T, NTT = self.P, self.T, self.NTT
        rg = [[0, 1, 2, 3], [4, 5, 6, 7]]
        own = self.dram("kvo%d" % l, [2 * NTT, 512, 1024], BF16)
        gat = self.dram("kvg%d" % l, [2 * NTT, 2048, 1024], BF16)
        rb = self.dram("rb%d" % l, [4, 2 * NTT, 512, 1024], BF16)
        self.pending_rot = l

    def emit_rotation(self, l):
        NTT = self.NTT
        gat = self.dram("kvg%d" % l, [2 * NTT, 2048, 1024], BF16)
        rb = self.dram("rb%d" % l, [4, 2 * NTT, 512, 1024], BF16)
        for j in range(4):
            self.rot_dma(rb[j], gat, 512, j, lambda a: a, [self.dbuf("kvg%d" % l, n) for n in range(2 * NTT)],
                         [self.dbuf("rb%d" % l)])

    def rot_dma(self, out, src, rows, j, sel, reads, writes):
        def fn(e):
            cache = self.__dict__.setdefault("_rotv", {})
            if (j, rows) not in cache:
                v = e.partition_id()
                cache[(j, rows)] = e.snap(((v % 4 + j) % 4) * rows, min_val=0, max_val=3 * rows)
            w = cache[(j, rows)]
            return e.dma_start(out=out, in_=sel(src[:, bass.ds(w, rows), :]))
        o = self.P._mk("sp", fn, reads, writes, ())
        P = self.P
        o.is_dma = True
        k = P.dcount["sp"]
        P.dcount["sp"] = k + 1
        o.sem = P.dsem["sp"][k % P.NDSEM]
        o.val = 16 * (k // P.NDSEM + 1)
        if k >= P.NDSEM:
            o.deps.append(P.dhist["sp"][k - P.NDSEM])
        P.dhist["sp"].append(o)
        P._upd(o, reads, writes)
        return o

    def nbank(self):
        k = self.bank_rr
        self.bank_rr = (k + 1) % 8
        return k

    def alloc_dense(self, es, l, with_y):
        nc = self.nc
        d = {}
        d["x"] = Tile(es.enter_context(self.sbt("xt", [128, 8, TT], F32)), 8, 2)
        d["h"] = Tile(es.enter_context(self.sbt("hT", [128, 8, TT], BF16)), 8, 2)
        d["a"] = Tile(es.enter_context(self.sbt("aT", [128, NFF, TT], BF16)), NFF, 2)
        d["w"] = [Tile(es.enter_context(self.sbt("ws%d" % i, [128, 4096], BF16))) for i in range(4)]
        d["wi"] = 0
        d["sq"] = Tile(es.enter_context(self.sbt("sq", [128, 8, 512], BF16)))
        d["rr"] = [Tile(es.enter_context(self.sbt("rr%d" % i, [128, 512], F32))) for i in range(2)]
        d["sg"] = [Tile(es.enter_context(self.sbt("sg%d" % i, [128, 512], F32))) for i in range(2)]
        d["sgi"] = 0
        d["ost"] = [Tile(es.enter_context(self.sbt("ost%d" % i, [128, 512], BF16))) for i in range(6)]
        d["osti"] = 0
        d["f1"] = [Tile(es.enter_context(self.sbt("f1_%d" % i, [128, 512], F32))) for i in range(4)]
        d["f1i"] = 0
        d["b1"] = [Tile(es.enter_context(self.sbt("b1_%d" % i, [128, 512], BF16))) for i in range(4)]
        d["b1i"] = 0
        if with_y:
            d["y"] = Tile(es.enter_context(self.sbt("yT", [128, 12, TT], BF16)))
            d["g"] = [Tile(es.enter_context(self.sbt("gr%d" % i, [128, 3, TT], BF16))) for i in range(2)]
        else:
            d["cs"] = Tile(es.enter_context(self.sbt("cst", [128, 2, TT], F32)))
        return d

    def ring(self, d, key):
        lst = d[key]
        i = d[key + "i"] if (key + "i") in d else 0
        d[key + "i"] = (i + 1) % len(lst)
        return lst[i]

    def wslot(self, d):
        i = d["wi"]
        d["wi"] = (i + 1) % len(d["w"])
        return d["w"][i]

    def mm_group(self, bank, n, lhs_fn, rhs_fn, reads):
        P = self.P
        ps = self.ps
        for k in range(n):
            lhsT = lhs_fn(k)
            rhs = rhs_fn(k)
            P.run("pe", (lambda e, lhsT=lhsT, rhs=rhs, k=k: e.matmul(ps[:, bank, :], lhsT, rhs, start=(k == 0), stop=(k == n - 1))),
                  reads=reads if (k == 0 or k == n - 1) else (), writes=[self.bank[bank]])

    def rmsnorm(self, d, gap, dst_fn=None):
        P = self.P
        ps = self.ps
        X, H, SQ = d["x"], d["h"], d["sq"]
        for sb in range(2):
            sl = slice(sb * 512, (sb + 1) * 512)
            P.run("act", lambda e, sl=sl: e.activation(out=SQ.t[:, :, :], in_=X.t[:, :, sl], func=AF.Square),
                  reads=[X.b(kc, sb) for kc in range(8)], writes=SQ.all())
            bk = self.nbank()
            self.mm_group(bk, 8, lambda k: self.ones[:, :], lambda k: SQ.t[:, k, :], reads=[self.cB] + SQ.all())
            rr = d["rr"][sb]
            P.run("act", lambda e, rr=rr, bk=bk: e.activation(out=rr.t[:, :], in_=ps[:, bk, :], func=AF.Sqrt, bias=EPS, scale=1.0 / D),
                  reads=[self.bank[bk]], writes=rr.all())
            P.run("dve", lambda e, rr=rr: e.reciprocal(rr.t[:, :], rr.t[:, :]), reads=rr.all(), writes=rr.all())
            for kc in range(8):
                if dst_fn is None:
                    out = H.t[:, kc, sl]
                    wr = [H.b(kc, sb)]
                else:
                    out = X.t[:, kc, sl]
                    wr = [X.b(kc, sb)]
                P.run("dve", lambda e, out=out, kc=kc, sl=sl, rr=rr: e.scalar_tensor_tensor(
                    out=out, in0=X.t[:, kc, sl], scalar=gap[:, kc:kc + 1], in1=rr.t[:, :], op0=ALU.mult, op1=ALU.mult),
                    reads=[X.b(kc, sb), self.cB] + rr.all(), writes=wr)

    def ffn(self, d, wi, wo):
        P = self.P
        ps = self.ps
        X, H, A = d["x"], d["h"], d["a"]
        wi, wi_r = wi
        wo, wo_r = wo
        for j in range(NFF):
            ws = self.wslot(d)
            P.dma("pool", ws.t[:, 0:2048], wi[j].rearrange("p k c -> p (k c)"), reads=wi_r, writes=ws.all())
            banks = {}
            for sb in range(2):
                for half in range(2):
                    bk = self.nbank()
                    banks[(sb, half)] = bk
                    self.mm_group(bk, 8,
                                  lambda k, half=half: ws.t[:, k * 256 + half * 128: k * 256 + half * 128 + 128],
                                  lambda k, sb=sb: H.t[:, k, sb * 512:(sb + 1) * 512],
                                  reads=ws.all() + [H.b(kc, sb) for kc in range(8)])
            for sb in range(2):
                sg = self.ring(d, "sg")
                bg, bu = banks[(sb, 0)], banks[(sb, 1)]
                P.run("act", lambda e, sg=sg, bg=bg: e.activation(out=sg.t[:, :], in_=ps[:, bg, :], func=AF.Silu),
                      reads=[self.bank[bg]], writes=sg.all())
                P.run("dve", lambda e, sg=sg, bu=bu, j=j, sb=sb: e.tensor_tensor(
                    A.t[:, j, sb * 512:(sb + 1) * 512], ps[:, bu, :], sg.t[:, :], ALU.mult),
                    reads=[self.bank[bu]] + sg.all(), writes=[A.b(j, sb)])
        for oc in range(8):
            ws = self.wslot(d)
            P.dma("pool", ws.t[:, 0:NFF * 128], wo[oc].rearrange("p k c -> p (k c)"), reads=wo_r, writes=ws.all())
            for sb in range(2):
                bk = self.nbank()
                self.mm_group(bk, NFF, lambda k: ws.t[:, k * 128:(k + 1) * 128],
                              lambda k, sb=sb: A.t[:, k, sb * 512:(sb + 1) * 512],
                              reads=ws.all() + [A.b(j, sb) for j in range(NFF)])
                sl = slice(sb * 512, (sb + 1) * 512)
                P.run("dve", lambda e, bk=bk, oc=oc, sl=sl: e.scalar_tensor_tensor(
                    out=X.t[:, oc, sl], in0=ps[:, bk, :], scalar=0.5, in1=X.t[:, oc, sl], op0=ALU.mult, op1=ALU.add),
                    reads=[self.bank[bk], X.b(oc, sb)], writes=[X.b(oc, sb)])

    def stage_dense1(self, l):
        nc, P, T = self.nc, self.P, self.T
        ps = self.ps
        xin = self.dram("xin%d" % l, [128, 8, T], F32)
        xres = self.dram("xres%d" % l, [128, 8, T], F32)
        wi = self.wdram("wf1i", l)
        wo = self.wdram("wf1o", l)
        wfm, wfm_r = self.wdram("wfm", l)
        wv, wv_r = self.wdram("wv", l)
        csd = self.dram("cs", [128, 2, T], F32)
        qd = [self.dram("q%s%d" % (m, l), [128, 4, T], BF16) for m in "abc"]
        if self.fused:
            kvo = self.dram("kvo%d" % l, [2 * self.NTT, 512, 1024], BF16)
            kad = kvo[0:self.NTT, 0:128, :].rearrange("t r c -> r t c")
            kbd = kvo[0:self.NTT, 128:256, :].rearrange("t r c -> r t c")
            kcd = kvo[0:self.NTT, 256:512, :].rearrange("t (k p) c -> p k t c", k=2)
            vd = kvo[self.NTT:2 * self.NTT].rearrange("t r (a c) -> (t r a) c", c=512)
        else:
            kad = self.dram("kao%d" % l, [128, T], BF16)
            kbd = self.dram("kbo%d" % l, [128, T], BF16)
            kcd = self.dram("kco%d" % l, [128, 2, T], BF16)
            vd = self.dram("vo%d" % l, [T, 512], BF16)
        gd = self.dram("gt%d" % l, [128, 24, T], BF16)
        vec = self.vec[l]
        with ExitStack() as es:
            d = self.alloc_dense(es, l, with_y=False)
            X, H = d["x"], d["h"]
            for tt in range(self.NTT):
                t0 = tt * TT
                if tt == 0:
                    P.dma("sp", X.t[:, :, :], xin[:, :, t0:t0 + TT], reads=[self.dbuf("xin%d" % l, tt)], writes=X.all())
                P.dma("sp", d["cs"].t[:, :, :], csd[:, :, t0:t0 + TT], writes=d["cs"].all())
                self.rmsnorm(d, vec[:, 0:8])
                self.ffn(d, wi, wo)
                P.dma("sp", xres[:, :, t0:t0 + TT], X.t[:, :, :], reads=X.all(), writes=[self.dbuf("xres%d" % l, tt)])
                self.rmsnorm(d, vec[:, 8:16])
                if tt + 1 < self.NTT:
                    P.dma("sp", X.t[:, :, :], xin[:, :, t0 + TT:t0 + 2 * TT], reads=[self.dbuf("xin%d" % l, tt + 1)], writes=X.all())
                for c in range(40):
                    ws = self.wslot(d)
                    P.dma("pool", ws.t[:, 0:1024], wfm[c].rearrange("p k c -> p (k c)"), reads=wfm_r, writes=ws.all())
                    for sb in range(2):
                        sl = slice(sb * 512, (sb + 1) * 512)
                        tsl = slice(t0 + sb * 512, t0 + (sb + 1) * 512)
                        bk = self.nbank()
                        self.mm_group(bk, 8, lambda k: ws.t[:, k * 128:(k + 1) * 128],
                                      lambda k, sl=sl: H.t[:, k, sl],
                                      reads=ws.all() + [H.b(kc, sb) for kc in range(8)])
                        ost = self.ring(d, "ost")
                        if c < 5:
                            isq = c < 4
                            sq1 = self.ring(d, "b1")
                            P.run("act", lambda e, sq1=sq1, bk=bk: e.activation(out=sq1.t[:, :], in_=ps[:, bk, :], func=AF.Square),
                                  reads=[self.bank[bk]], writes=sq1.all())
                            b2 = self.nbank()
                            self.mm_group(b2, 1, lambda k: self.blk[:, :], lambda k, sq1=sq1: sq1.t[:, :], reads=[self.cB] + sq1.all())
                            rt = self.ring(d, "f1")
                            if isq:
                                P.run("act", lambda e, rt=rt, b2=b2: e.activation(out=rt.t[:, :], in_=ps[:, b2, :], func=AF.Sqrt, bias=64.0 * EPS, scale=1.0),
                                      reads=[self.bank[b2]], writes=rt.all())
                            else:
                                P.run("act", lambda e, rt=rt, b2=b2: e.activation(out=rt.t[:, :], in_=ps[:, b2, :], func=AF.Sqrt, bias=EPS, scale=1.0 / 64.0),
                                      reads=[self.bank[b2]], writes=rt.all())
                            P.run("dve", lambda e, rt=rt: e.reciprocal(rt.t[:, :], rt.t[:, :]), reads=rt.all(), writes=rt.all())
                            xn = self.ring(d, "b1")
                            gcol = 24 if isq else 25
                            P.run("dve", lambda e, xn=xn, bk=bk, rt=rt, gcol=gcol: e.scalar_tensor_tensor(
                                out=xn.t[:, :], in0=ps[:, bk, :], scalar=vec[:, gcol:gcol + 1], in1=rt.t[:, :], op0=ALU.mult, op1=ALU.mult),
                                reads=[self.bank[bk], self.cB] + rt.all(), writes=xn.all())
                            b3 = self.nbank()
                            self.mm_group(b3, 1, lambda k: self.rmat[:, :], lambda k, xn=xn: xn.t[:, :], reads=[self.cB] + xn.all())
                            t1 = self.ring(d, "f1")
                            P.run("dve", lambda e, t1=t1, xn=xn, sl=sl: e.tensor_tensor(t1.t[:, :], xn.t[:, :], d["cs"].t[:, 0, sl], ALU.mult),
                                  reads=xn.all() + d["cs"].all(), writes=t1.all())
                            t2 = self.ring(d, "f1")
                            P.run("dve", lambda e, t2=t2, b3=b3, sl=sl: e.tensor_tensor(t2.t[:, :], ps[:, b3, :], d["cs"].t[:, 1, sl], ALU.mult),
                                  reads=[self.bank[b3]] + d["cs"].all(), writes=t2.all())
                            P.run("dve", lambda e, ost=ost, t1=t1, t2=t2: e.tensor_tensor(ost.t[:, :], t1.t[:, :], t2.t[:, :], ALU.add),
                                  reads=t1.all() + t2.all(), writes=ost.all())
                            dst = qd[0][:, c, tsl] if isq else (kad[:, tt, sl] if self.fused else kad[:, tsl])
                            nm = ("qa%d" % l) if isq else (("kvo%d" if self.fused else "kao%d") % l)
                        elif c < 16:
                            isq = (5 <= c < 9) or (10 <= c < 14)
                            sc = 0.125 if isq else 1.0
                            P.run("act", lambda e, ost=ost, bk=bk, sc=sc: e.activation(out=ost.t[:, :], in_=ps[:, bk, :], func=AF.Copy, scale=sc),
                                  reads=[self.bank[bk]], writes=ost.all())
                            if c < 9:
                                dst, nm = qd[1][:, c - 5, tsl], "qb%d" % l
                            elif c == 9:
                                dst, nm = (kbd[:, tt, sl] if self.fused else kbd[:, tsl]), ("kvo%d" if self.fused else "kbo%d") % l
                            elif c < 14:
                                dst, nm = qd[2][:, c - 10, tsl], "qc%d" % l
                            else:
                                dst, nm = (kcd[:, c - 14, tt, sl] if self.fused else kcd[:, c - 14, tsl]), ("kvo%d" if self.fused else "kco%d") % l
                        else:
                            P.run("act", lambda e, ost=ost, bk=bk: e.activation(out=ost.t[:, :], in_=ps[:, bk, :], func=AF.Sigmoid),
                                  reads=[self.bank[bk]], writes=ost.all())
                            dst, nm = gd[:, c - 16, tsl], "gt%d" % l
                        P.dma("sp", dst, ost.t[:, :], reads=ost.all(), writes=[self.dbuf(nm, tt)])
                ws = self.wslot(d)
                P.dma("pool", ws.t[:, 0:4096], wv.rearrange("p k c -> p (k c)"), reads=wv_r, writes=ws.all())
                for tb in range(TT // 128):
                    bk = self.nbank()
                    self.mm_group(bk, 8, lambda k, tb=tb: H.t[:, k, tb * 128:(tb + 1) * 128],
                                  lambda k: ws.t[:, k * 512:(k + 1) * 512],
                                  reads=ws.all() + [H.b(kc, tb // 4) for kc in range(8)])
                    ost = self.ring(d, "ost")
                    P.run("act", lambda e, ost=ost, bk=bk: e.activation(out=ost.t[:, :], in_=ps[:, bk, :], func=AF.Copy),
                          reads=[self.bank[bk]], writes=ost.all())
                    P.dma("sp", vd[t0 + tb * 128:t0 + (tb + 1) * 128, :], ost.t[:, :], reads=ost.all(), writes=[self.dbuf(("kvo%d" if self.fused else "vo%d") % l, tt)])
                if self.fused:
                    self.gather_tile(l, tt)

    def stage_dense2(self, l):
        nc, P, T = self.nc, self.P, self.T
        ps = self.ps
        xres = self.dram("xres%d" % l, [128, 8, T], F32)
        last = self.final and l == max(s[1] for s in self.stages)
        if last:
            xout = self.dram("outT", [128, 8, T], F32)
            xoname = "outT"
        else:
            xout = self.dram("xin%d" % (l + 1), [128, 8, T], F32)
            xoname = "xin%d" % (l + 1)
        wi = self.wdram("wf2i", l)
        wo = self.wdram("wf2o", l)
        wbr, wbr_r = self.wdram("wbr", l)
        wout, wout_r = self.wdram("wo", l)
        yd = self.dram("yT%d" % l, [128, 12, T], BF16)
        gd = self.dram("gt%d" % l, [128, 24, T], BF16)
        vec = self.vec[l]
        with ExitStack() as es:
            d = self.alloc_dense(es, l, with_y=True)
            X, H, Y = d["x"], d["h"], d["y"]
            for tt in range(self.NTT):
                t0 = tt * TT
                P.dma("sp", X.t[:, :, :], xres[:, :, t0:t0 + TT], reads=[self.dbuf("xres%d" % l, tt)], writes=X.all())
                if tt == 0:
                    P.dma("sp", Y.t[:, :, :], yd[:, :, t0:t0 + TT], reads=[self.dbuf("yT%d" % l)], writes=Y.all())
                for oc in range(8):
                    G = d["g"][oc % 2]
                    P.dma("sp", G.t[:, :, :], gd.rearrange("p (n o) t -> p n o t", n=3)[:, :, oc, t0:t0 + TT],
                          reads=[self.dbuf("gt%d" % l, tt)], writes=G.all())
                    ws = self.wslot(d)
                    P.dma("pool", ws.t[:, 0:1536], wbr[oc].rearrange("p k c -> p (k c)"), reads=wbr_r, writes=ws.all())
                    for sb in range(2):
                        sl = slice(sb * 512, (sb + 1) * 512)
                        acc = None
                        for n in range(3):
                            bk = self.nbank()
                            self.mm_group(bk, 4, lambda k, n=n: ws.t[:, (n * 4 + k) * 128:(n * 4 + k + 1) * 128],
                                          lambda k, n=n, sl=sl: Y.t[:, n * 4 + k, sl], reads=ws.all() + Y.all())
                            tmp = self.ring(d, "f1")
                            P.run("dve", lambda e, tmp=tmp, bk=bk, n=n, sl=sl, G=G: e.tensor_tensor(tmp.t[:, :], ps[:, bk, :], G.t[:, n, sl], ALU.mult),
                                  reads=[self.bank[bk]] + G.all(), writes=tmp.all())
                            if n == 0:
                                acc = tmp
                            elif n == 1:
                                P.run("pool", lambda e, acc=acc, tmp=tmp: e.tensor_tensor(acc.t[:, :], acc.t[:, :], tmp.t[:, :], ALU.add),
                                      reads=acc.all() + tmp.all(), writes=acc.all())
                            else:
                                P.run("pool", lambda e, acc=acc, tmp=tmp, oc=oc, sl=sl: e.tensor_tensor(H.t[:, oc, sl], acc.t[:, :], tmp.t[:, :], ALU.add),
                                      reads=acc.all() + tmp.all(), writes=[H.b(oc, sb)])
                if tt + 1 < self.NTT:
                    P.dma("sp", Y.t[:, :, :], yd[:, :, t0 + TT:t0 + 2 * TT], reads=[self.dbuf("yT%d" % l)], writes=Y.all())
                for oc in range(8):
                    ws = self.wslot(d)
                    P.dma("pool", ws.t[:, 0:1024], wout[oc].rearrange("p k c -> p (k c)"), reads=wout_r, writes=ws.all())
                    for sb in range(2):
                        sl = slice(sb * 512, (sb + 1) * 512)
                        bk = self.nbank()
                        self.mm_group(bk, 8, lambda k: ws.t[:, k * 128:(k + 1) * 128], lambda k, sl=sl: H.t[:, k, sl],
                                      reads=ws.all() + [H.b(kc, sb) for kc in range(8)])
                        P.run("dve", lambda e, bk=bk, oc=oc, sl=sl: e.tensor_tensor(X.t[:, oc, sl], ps[:, bk, :], X.t[:, oc, sl], ALU.add),
                              reads=[self.bank[bk], X.b(oc, sb)], writes=[X.b(oc, sb)])
                self.rmsnorm(d, vec[:, 16:24])
                self.ffn(d, wi, wo)
                if last:
                    self.rmsnorm(d, self.gfin, dst_fn=True)
                o = P.dma("sp", xout[:, :, t0:t0 + TT], X.t[:, :, :], reads=X.all(), writes=[self.dbuf(xoname, tt)])
                if last:
                    P.out_dmas.append(o)

    def stage_attn_a(self, l):
        nc, P, T, S = self.nc, self.P, self.T, self.S
        ps = self.ps
        NKT, NQ = self.NKT, self.NQ
        qd = self.dram("qa%d" % l, [128, 4, T], BF16)
        yd = self.dram("yT%d" % l, [128, 12, T], BF16)
        with ExitStack() as es:
            K = Tile(es.enter_context(self.sbt("kA", [128, S], BF16)), 4)
            V = Tile(es.enter_context(self.sbt("vA", [128, 2, NKT, 128], BF16)), 2, 4 * self.NTT)
            Q = Tile(es.enter_context(self.sbt("qA", [128, 4, T], BF16)))
            PT = [Tile(es.enter_context(self.sbt("ptA%d" % i, [128, 1024], BF16))) for i in range(3)]
            RC = Tile(es.enter_context(self.sbt("rcA", [64, 512], F32)))
            YS = [Tile(es.enter_context(self.sbt("ysA%d" % i, [64, 512], BF16))) for i in range(2)]
            kb = self.dbuf("kaf%d" % l)
            vb = self.dbuf("vf%d" % l)
            P.dma("sp", Q.t[:, :, :], qd, reads=[self.dbuf("qa%d" % l, tt) for tt in range(self.NTT)], writes=Q.all())
            P.run("pool", lambda e: e.memset(V.t[:, :, :, 64:128], 1.0), writes=V.all())
            NT_ = self.NT
            NTT_ = self.NTT
            if self.fused:
                gat = self.dram("kvg%d" % l, [2 * NTT_, 2048, 1024], BF16)
                grd = [self.dbuf("kvg%d" % l, n) for n in range(2 * NTT_)]
                for j in range(4):
                    P.dma("sp", K.t[:, j * T:(j + 1) * T].rearrange("p (t c) -> p t c", c=1024),
                          gat[0:NTT_, j * 512:j * 512 + 128, :].rearrange("t r c -> r t c"), reads=grd, writes=[K.b(j)])
                    for tt in range(NTT_):
                        vv = gat[NTT_ + tt, j * 512:(j + 1) * 512, :].rearrange("r (a c) -> (r a) c", c=512)
                        for kv in range(2):
                            P.dma("sp", V.t[:, kv, j * NT_ + tt * 8:j * NT_ + (tt + 1) * 8, 0:64],
                                  vv.rearrange("(k p) c -> p k c", p=128)[:, :, kv * 64:(kv + 1) * 64],
                                  reads=grd, writes=[V.b(kv, j * NTT_ + tt)])
                self.emit_rotation(l)
            else:
                for j in range(4):
                    P.dma("sp", K.t[:, j * T:(j + 1) * T].rearrange("p (t c) -> p t c", c=1024), self.seg_k(l, "a", j), reads=self.kv_rbuf(l), writes=[K.b(j)])
                    for kv in range(2):
                        P.dma("sp", V.t[:, kv, j * NT_:(j + 1) * NT_, 0:64],
                              self.seg_v(l, j).rearrange("(k p) c -> p k c", p=128)[:, :, kv * 64:(kv + 1) * 64],
                              reads=self.kv_rbuf(l), writes=[V.b(kv, j * NTT_ + t_) for t_ in range(NTT_)])
            ysc = [0]

            def unit_a(g, qt):
                        qs = slice(qt * 512, (qt + 1) * 512)
                        ob = [6, 7]

                        def qk(i):
                            sbk = 2 * (i % 3)
                            for hh in range(2):
                                pr = slice(hh * 64, (hh + 1) * 64)
                                P.run("pe", lambda e, sbk=sbk, hh=hh, pr=pr, i=i: e.matmul(
                                    ps[:, sbk + hh, :], K.t[pr, i * 128:(i + 1) * 128], Q.t[pr, g, qs], start=True, stop=True),
                                    reads=(K.all() + Q.all()), writes=[self.bank[sbk + hh]])

                        def ex(i):
                            sbk = 2 * (i % 3)
                            pt = PT[i % 3]
                            P.run("act", lambda e, sbk=sbk, pt=pt: e.activation(
                                out=pt.t[:, :], in_=ps[:, sbk:sbk + 2, :].rearrange("p a b -> p (a b)"), func=AF.Exp),
                                reads=[self.bank[sbk], self.bank[sbk + 1]], writes=pt.all())

                        def pv(i):
                            pt = PT[i % 3]
                            for hh in range(2):
                                P.run("pe", lambda e, pt=pt, hh=hh, i=i: e.matmul(
                                    ps[:, ob[hh], :], V.t[:, hh, i, :], pt.t[:, hh * 512:(hh + 1) * 512], start=(i == 0), stop=(i == NKT - 1)),
                                    reads=(V.all() + pt.all()), writes=[self.bank[ob[hh]]])

                        qk(0)
                        qk(1)
                        for i in range(NKT):
                            ex(i)
                            if i + 2 < NKT:
                                qk(i + 2)
                            pv(i)
                        for hh in range(2):
                            pr = slice(hh * 64, (hh + 1) * 64)
                            P.run("dve", lambda e, hh=hh: e.reciprocal(RC.t[:, :], ps[64:128, ob[hh], :]),
                                  reads=[self.bank[ob[hh]]], writes=RC.all())
                            ys = YS[ysc[0] % 2]
                            ysc[0] += 1
                            P.run("dve", lambda e, hh=hh, ys=ys: e.tensor_tensor(ys.t[:, :], ps[0:64, ob[hh], :], RC.t[:, :], ALU.mult),
                                  reads=[self.bank[ob[hh]]] + RC.all(), writes=ys.all())
                            P.dma("sp", yd[pr, g, qs], ys.t[:, :], reads=ys.all(), writes=[self.dbuf("yT%d" % l)])

            for g in range(4):
                for qt in range(NQ):
                    unit_a(g, qt)

    def stage_attn_c(self, l):
        nc, P, T, S = self.nc, self.P, self.T, self.S
        ps = self.ps
        NKT, NQ, NT = self.NKT, self.NQ, self.NT
        qd = self.dram("qc%d" % l, [128, 4, T], BF16)
        yd = self.dram("yT%d" % l, [128, 12, T], BF16)
        gcd = self.dram("gc", [4, 128, 1152], F32)
        ced = self.dram("cedge", [4, 2, 128, 512], F32)
        cbd = self.dram("cb", [128, 20], F32)
        vec = self.vec[l]
        lam_init = 0.8 - 0.6 * math.exp(-0.3 * l)
        with ExitStack() as es:
            sbt = lambda n, s, dt: es.enter_context(self.sbt(n, s, dt))
            K = Tile(sbt("kC", [128, S], BF16), 4)
            V = Tile(sbt("vC", [128, NKT, 128], BF16), 4)
            Q = Tile(sbt("qC", [128, 2, T], BF16))
            GC = Tile(sbt("gC", [128, 4, 1152], F32))
            CE = Tile(sbt("ceC", [128, 8, 512], F32))
            CB = Tile(sbt("cbC", [128, 20], F32))
            LM = Tile(sbt("lmC", [128, 8], F32))
            LT = Tile(sbt("ltC", [128, 64], F32))
            PT = [Tile(sbt("ptC%d" % i, [128, 1024], BF16)) for i in range(4)]
            SS = [Tile(sbt("ssC%d" % i, [128, 1024], F32)) for i in range(2)]
            RL = [Tile(sbt("rlC%d" % i, [128, 512], F32)) for i in range(2)]
            O1 = Tile(sbt("o1C", [128, 512], F32))
            O2 = Tile(sbt("o2C", [128, 512], F32))
            SQ = Tile(sbt("sqC", [128, 512], BF16))
            RT = Tile(sbt("rtC", [128, 512], F32))
            YS = [Tile(sbt("ysC%d" % i, [128, 512], BF16)) for i in range(2)]
            ACCT = sbt("accC", [128, 1024], F32)
            ACCB = [Buf(), Buf()]
            ONESF = Tile(sbt("onesfC", [128, 128], F32))
            P.run("pool", lambda e: e.memset(ONESF.t[:, :], 1.0), writes=ONESF.all())
            EPSC = Tile(sbt("epsC", [128, 1], F32))
            P.run("pool", lambda e: e.memset(EPSC.t[:, :], EPS), writes=EPSC.all())
            cB = Buf()
            for h in range(4):
                P.dma("sp", GC.t[:, h, :], gcd[h], writes=[cB])
                for e2 in range(2):
                    P.dma("sp", CE.t[:, h * 2 + e2, :], ced[h, e2], writes=[cB])
            P.dma("sp", CB.t[:, :], cbd, writes=[cB])
            for i in range(2):
                P.run("dve", lambda e, i=i: e.tensor_tensor(LT.t[:, :], vec[:, 40 + 128 * i:104 + 128 * i], vec[:, 104 + 128 * i:168 + 128 * i], ALU.mult),
                      reads=[self.cB], writes=LT.all())
                P.run("dve", lambda e, i=i: e.reduce_sum(LM.t[:, i:i + 1], LT.t[:, :], mybir.AxisListType.X), reads=LT.all(), writes=LM.all())
            P.run("act", lambda e: e.activation(out=LM.t[:, 0:2], in_=LM.t[:, 0:2], func=AF.Exp), reads=LM.all(), writes=LM.all())
            P.run("dve", lambda e: e.tensor_tensor(LM.t[:, 2:3], LM.t[:, 1:2], LM.t[:, 0:1], ALU.subtract), reads=LM.all(), writes=LM.all())
            P.run("dve", lambda e: e.tensor_scalar(LM.t[:, 2:3], LM.t[:, 2:3], -lam_init, None, ALU.add), reads=LM.all(), writes=LM.all())
            P.run("dve", lambda e: e.tensor_scalar(LM.t[:, 3:4], vec[:, 26:27], 1.0 - lam_init, None, ALU.mult), reads=LM.all() + [self.cB], writes=LM.all())
            ysc = [0]

            def unit_c(kv, hl, qt):
                        h = 2 * kv + hl
                        qs = slice(qt * 512, (qt + 1) * 512)
                        ob = [4, 5]
                        lb = [6, 7]

                        def kind(i):
                            if i < NT:
                                dl = i - 4 * qt
                                if -1 <= dl <= 4:
                                    off = 512 - 128 * dl
                                    return ("near", GC.t[:, h, off:off + 512])
                                return ("far", h * 5 + (0 if dl < 0 else 1))
                            if i == NT and qt == NQ - 1:
                                return ("near", CE.t[:, h * 2 + 1, :])
                            if i == NKT - 1 and qt == 0:
                                return ("near", CE.t[:, h * 2 + 0, :])
                            return ("far", h * 5 + 1 + i // NT)

                        def qk(i):
                            sbk = 2 * (i % 2)
                            for m in range(2):
                                mr = slice(m * 64, (m + 1) * 64)
                                P.run("pe", lambda e, sbk=sbk, m=m, mr=mr, i=i: e.matmul(
                                    ps[:, sbk + m, :], K.t[mr, i * 128:(i + 1) * 128], Q.t[mr, hl, qs], start=True, stop=True),
                                    reads=(K.all() + Q.all()), writes=[self.bank[sbk + m]])

                        def ex(i):
                            sbk = 2 * (i % 2)
                            pt = PT[i % 4]
                            kd = kind(i)
                            if kd[0] == "far":
                                col = kd[1]
                                P.run("act", lambda e, sbk=sbk, pt=pt, col=col: e.activation(
                                    out=pt.t[:, :], in_=ps[:, sbk:sbk + 2, :].rearrange("p a b -> p (a b)"), func=AF.Exp, bias=CB.t[:, col:col + 1]),
                                    reads=[self.bank[sbk], self.bank[sbk + 1], cB], writes=pt.all())
                            else:
                                bt = kd[1]
                                ss = SS[i % 2]
                                for m in range(2):
                                    P.run("dve", lambda e, sbk=sbk, m=m, ss=ss, bt=bt: e.tensor_tensor(
                                        ss.t[:, m * 512:(m + 1) * 512], ps[:, sbk + m, :], bt, ALU.add),
                                        reads=[self.bank[sbk + m], cB], writes=ss.all())
                                P.run("act", lambda e, ss=ss, pt=pt: e.activation(out=pt.t[:, :], in_=ss.t[:, :], func=AF.Exp),
                                      reads=ss.all(), writes=pt.all())

                        def pv(i):
                            pt = PT[i % 4]
                            for m in range(2):
                                P.run("pe", lambda e, pt=pt, m=m, i=i: e.matmul(
                                    ps[:, ob[m], :], V.t[:, i, :], pt.t[:, m * 512:(m + 1) * 512], start=(i == 0), stop=(i == NKT - 1)),
                                    reads=(V.all() + pt.all()), writes=[self.bank[ob[m]]])
                            if i % 4 == 3:
                                for m in range(2):
                                    P.run("pe", lambda e, pt=pt, m=m, i=i: e.matmul(
                                        ps[:, lb[m], :], self.ones[:, :], pt.t[:, m * 512:(m + 1) * 512], start=(i == 3), stop=False),
                                        reads=[self.cB] + pt.all(), writes=[self.bank[lb[m]]])
                            elif i == 0:
                                P.run("dve", lambda e, pt=pt: e.tensor_copy(ACCT[:, :], pt.t[:, :]), reads=pt.all(), writes=ACCB)
                            else:
                                P.run("dve", lambda e, pt=pt: e.tensor_tensor(ACCT[:, :], ACCT[:, :], pt.t[:, :], ALU.add),
                                      reads=pt.all() + ACCB, writes=ACCB)

                        qk(0)
                        qk(1)
                        for i in range(NKT):
                            ex(i)
                            if i + 2 < NKT:
                                qk(i + 2)
                            pv(i)
                        for m in range(2):
                            P.run("pe", lambda e, m=m: e.matmul(ps[:, lb[m], :], ONESF.t[:, :], ACCT[:, m * 512:(m + 1) * 512], start=False, stop=True),
                                  reads=ONESF.all() + ACCB, writes=[self.bank[lb[m]]])
                        for m in range(2):
                            P.run("act", lambda e, m=m: e.activation(out=RL[m].t[:, :], in_=ps[:, lb[m], :], func=AF.Ln), reads=[self.bank[lb[m]]], writes=RL[m].all())
                            P.run("act", lambda e, m=m: e.activation(out=RL[m].t[:, :], in_=RL[m].t[:, :], func=AF.Exp, scale=-1.0), reads=RL[m].all(), writes=RL[m].all())
                        P.run("dve", lambda e: e.tensor_tensor(O1.t[:, :], ps[:, ob[0], :], RL[0].t[:, :], ALU.mult),
                              reads=[self.bank[ob[0]]] + RL[0].all(), writes=O1.all())
                        P.run("dve", lambda e: e.tensor_tensor(O2.t[:, :], ps[:, ob[1], :], RL[1].t[:, :], ALU.mult),
                              reads=[self.bank[ob[1]]] + RL[1].all(), writes=O2.all())
                        P.run("dve", lambda e: e.scalar_tensor_tensor(out=O1.t[:, :], in0=O2.t[:, :], scalar=LM.t[:, 2:3], in1=O1.t[:, :], op0=ALU.mult, op1=ALU.add),
                              reads=O1.all() + O2.all() + LM.all(), writes=O1.all())
                        P.run("act", lambda e: e.activation(out=SQ.t[:, :], in_=O1.t[:, :], func=AF.Square), reads=O1.all(), writes=SQ.all())
                        b2 = 0
                        self.mm_group(b2, 1, lambda k: self.ones[:, :], lambda k: SQ.t[:, :], reads=[self.cB] + SQ.all())
                        P.run("act", lambda e, b2=b2: e.activation(out=RT.t[:, :], in_=ps[:, b2, :], func=AF.Ln, bias=EPSC.t[:, 0:1], scale=1.0 / 128.0),
                              reads=[self.bank[b2]] + EPSC.all(), writes=RT.all())
                        P.run("act", lambda e: e.activation(out=RT.t[:, :], in_=RT.t[:, :], func=AF.Exp, scale=-0.5), reads=RT.all(), writes=RT.all())
                        ys = YS[ysc[0] % 2]
                        ysc[0] += 1
                        P.run("dve", lambda e, ys=ys: e.scalar_tensor_tensor(out=ys.t[:, :], in0=O1.t[:, :], scalar=LM.t[:, 3:4], in1=RT.t[:, :], op0=ALU.mult, op1=ALU.mult),
                              reads=O1.all() + RT.all() + LM.all(), writes=ys.all())
                        P.dma("sp", yd[:, 8 + h, qs], ys.t[:, :], reads=ys.all(), writes=[self.dbuf("yT%d" % l)])

            for kv in range(2):
                for j in range(4):
                    P.dma("sp", K.t[:, j * T:(j + 1) * T].rearrange("p (t c) -> p t c", c=1024), self.seg_k(l, "c", j, kv), reads=self.kv_rbuf(l), writes=[K.b(j)])
                    P.dma("sp", V.t[:, j * NT:(j + 1) * NT, :],
                          self.seg_v(l, j).rearrange("(k p) c -> p k c", p=128)[:, :, 256 + kv * 128:256 + (kv + 1) * 128],
                          reads=self.kv_rbuf(l), writes=[V.b(j)])
                P.dma("sp", Q.t[:, :, :], qd[:, 2 * kv:2 * kv + 2, :], reads=[self.dbuf("qc%d" % l, tt) for tt in range(self.NTT)], writes=Q.all())
                for hl in range(2):
                    for qt in range(NQ):
                        unit_c(kv, hl, qt)

    def stage_attn_b(self, l):
        nc, P, T, S = self.nc, self.P, self.T, self.S
        ps = self.ps
        NKT, NT = self.NKT, self.NT
        qd = self.dram("qb%d" % l, [128, 4, T], BF16)
        yd = self.dram("yT%d" % l, [128, 12, T], BF16)
        bbd = self.dram("bb", [10, 128, 512], F32)
        vec = self.vec[l]
        with ExitStack() as es:
            sbt = lambda n, s, dt: es.enter_context(self.sbt(n, s, dt))
            K = Tile(sbt("kB", [128, NT + 2, 128], BF16))
            V = Tile(sbt("vB", [128, 2, NT + 2, 128], BF16))
            Q = Tile(sbt("qB", [128, 4, T], BF16))
            BB = Tile(sbt("bbB", [128, 10, 512], F32))
            ES = Tile(sbt("esB", [128, 8], F32))
            SS = [Tile(sbt("ssB%d" % i, [128, 512], F32)) for i in range(3)]
            PT = [Tile(sbt("ptB%d" % i, [128, 512], BF16)) for i in range(3)]
            DEN = Tile(sbt("denB", [128, 512], F32))
            RC = Tile(sbt("rcB", [64, 512], F32))
            YS = [Tile(sbt("ysB%d" % i, [64, 512], BF16)) for i in range(2)]
            cB = Buf()
            kb = self.dbuf("kbf%d" % l)
            vb = self.dbuf("vf%d" % l)
            P.run("pool", lambda e: e.memset(V.t[:, :, :, 64:128], 1.0), writes=V.all())
            kr = self.kv_rbuf(l)
            k3, k0, k1 = self.seg_k(l, "b", 3), self.seg_k(l, "b", 0), self.seg_k(l, "b", 1)
            v3, v0, v1 = self.seg_v(l, 3), self.seg_v(l, 0), self.seg_v(l, 1)
            P.dma("sp", K.t[:, 0, :], k3[:, self.NTT - 1, 896:1024], reads=kr, writes=K.all())
            P.dma("sp", K.t[:, 1:NT + 1, :].rearrange("p (t k) c -> p t k c", k=8), k0.rearrange("p t (k c) -> p t k c", c=128), reads=kr, writes=K.all())
            P.dma("sp", K.t[:, NT + 1, :], k1[:, 0, 0:128], reads=kr, writes=K.all())
            for kv in range(2):
                cs_ = slice(128 + kv * 64, 128 + (kv + 1) * 64)
                P.dma("sp", V.t[:, kv, 0, 0:64], v3[(NT - 1) * 128:NT * 128, cs_], reads=kr, writes=V.all())
                P.dma("sp", V.t[:, kv, 1:NT + 1, 0:64], v0.rearrange("(k p) c -> p k c", p=128)[:, :, cs_], reads=kr, writes=V.all())
                P.dma("sp", V.t[:, kv, NT + 1, 0:64], v1[0:128, cs_], reads=kr, writes=V.all())
            P.dma("sp", Q.t[:, :, :], qd, reads=[self.dbuf("qb%d" % l, tt) for tt in range(self.NTT)], writes=Q.all())
            for i in range(10):
                P.dma("sp", BB.t[:, i, :], bbd[i], writes=[cB])
            BH = Tile(sbt("bhB", [128, 10, 512], BF16))
            BL = Tile(sbt("blB", [128, 10, 512], BF16))
            BT = Tile(sbt("btB", [128, 512], F32))
            for i in range(10):
                P.run("dve", lambda e, i=i: e.tensor_copy(BH.t[:, i, :], BB.t[:, i, :]), reads=[cB], writes=BH.all())
                P.run("dve", lambda e, i=i: e.tensor_tensor(BT.t[:, :], BB.t[:, i, :], BH.t[:, i, :], ALU.subtract), reads=[cB] + BH.all(), writes=BT.all())
                P.run("dve", lambda e, i=i: e.tensor_copy(BL.t[:, i, :], BT.t[:, :]), reads=BT.all(), writes=BL.all())
            P.run("act", lambda e: e.activation(out=ES.t[:, :], in_=vec[:, 32:40], func=AF.Exp), reads=[self.cB], writes=[cB])
            ysc = [0]

            def unit_b(kv, n):
                    pr = slice(kv * 64, (kv + 1) * 64)
                    ob = 7
                    for dd in range(3):
                        bk = dd * 2
                        var = dd
                        if n == 0 and dd == 0:
                            var = 3
                        if n == NT - 1 and dd == 2:
                            var = 4
                        P.run("pe", lambda e, bk=bk, dd=dd, n=n: e.matmul(
                            ps[:, bk, :], K.t[pr, n + dd, :], Q.t[pr, :, n * 128:(n + 1) * 128], start=True, stop=False),
                            reads=K.all() + Q.all(), writes=[self.bank[bk]])
                        P.run("pe", lambda e, bk=bk, var=var: e.matmul(ps[:, bk, :], self.ident[:, :], BH.t[:, kv * 5 + var, :], start=False, stop=False),
                              reads=[self.cB] + BH.all(), writes=[self.bank[bk]])
                        P.run("pe", lambda e, bk=bk, var=var: e.matmul(ps[:, bk, :], self.ident[:, :], BL.t[:, kv * 5 + var, :], start=False, stop=True),
                              reads=[self.cB] + BL.all(), writes=[self.bank[bk]])
                        pt = PT[dd]
                        P.run("act", lambda e, bk=bk, pt=pt: e.activation(out=pt.t[:, :], in_=ps[:, bk, :], func=AF.Exp), reads=[self.bank[bk]], writes=pt.all())
                    for dd in range(3):
                        pt = PT[dd]
                        P.run("pe", lambda e, pt=pt, dd=dd, n=n: e.matmul(ps[:, ob, :], V.t[:, kv, n + dd, :], pt.t[:, :], start=(dd == 0), stop=(dd == 2)),
                              reads=V.all() + pt.all(), writes=[self.bank[ob]])
                    for g in range(4):
                        gs = slice(g * 128, (g + 1) * 128)
                        P.run("dve", lambda e, g=g, gs=gs: e.tensor_scalar(DEN.t[64:128, gs], ps[64:128, ob, gs], ES.t[64:128, kv * 4 + g:kv * 4 + g + 1], None, ALU.add),
                              reads=[self.bank[ob], cB], writes=DEN.all())
                    P.run("dve", lambda e: e.reciprocal(RC.t[:, :], DEN.t[64:128, :]), reads=DEN.all(), writes=RC.all())
                    ys = YS[ysc[0] % 2]
                    ysc[0] += 1
                    P.run("dve", lambda e, ys=ys: e.tensor_tensor(ys.t[:, :], ps[0:64, ob, :], RC.t[:, :], ALU.mult),
                          reads=[self.bank[ob]] + RC.all(), writes=ys.all())
                    P.dma("sp", yd[pr, 4:8, n * 128:(n + 1) * 128], ys.t[:, :].rearrange("p (g q) -> p g q", g=4), reads=ys.all(), writes=[self.dbuf("yT%d" % l)])

            for kv in range(2):
                for n in range(NT):
                    unit_b(kv, n)


def _t5_bucket_np(rel):
    import jax
    import jax.numpy as jnp
    with jax.default_device(jax.devices("cpu")[0]):
        rel = jnp.asarray(rel, dtype=jnp.int32)
        nb = 16
        max_exact = 8
        side = jnp.where(rel > 0, nb, 0)
        n = jnp.abs(rel)
        nf = jnp.maximum(n, 1).astype(jnp.float32)
        large = max_exact + (jnp.log(nf / max_exact) / math.log(128 / max_exact) * (nb - max_exact)).astype(jnp.int32)
        large = jnp.minimum(large, nb - 1)
        return np.asarray(side + jnp.where(n < max_exact, n, large))


def _rope_tables(S):
    rows = S // 64
    row_ids = np.repeat(np.arange(rows), 64).astype(np.float32)
    col_ids = np.tile(np.arange(64), rows).astype(np.float32)
    freqs = (np.float32(10000.0) ** (-np.arange(0, 32, 2, dtype=np.float32) / np.float32(32))).astype(np.float32)
    ang_r = row_ids[:, None] * freqs
    ang_c = col_ids[:, None] * freqs
    cos = np.concatenate([np.cos(ang_r), np.cos(ang_r), np.cos(ang_c), np.cos(ang_c)], axis=1)
    sin = np.concatenate([np.sin(ang_r), np.sin(ang_r), np.sin(ang_c), np.sin(ang_c)], axis=1)
    return cos.astype(np.float32), sin.astype(np.float32)


def _rmat():
    R = np.zeros((64, 64), np.float32)
    for i in range(64):
        if (i % 32) < 16:
            R[i, i + 16] = -1.0
        else:
            R[i, i - 16] = 1.0
    full = np.zeros((128, 128), np.float32)
    full[0:64, 0:64] = R
    full[64:128, 64:128] = R
    return np.ascontiguousarray(full.T)


def _fm_w(w, ncols_chunks):
    K, C = w.shape
    return np.ascontiguousarray(w.reshape(K // 128, 128, C // 128, 128).transpose(2, 1, 0, 3))


def prep_layer_weights(inp, l):
    out = {}
    for i, nm in ((1, "w_ffn1"), (2, "w_ffn2")):
        wi = inp[nm + "_in"][l]
        g = wi[:, :DFF].reshape(8, 128, NFF, 128)
        u = wi[:, DFF:].reshape(8, 128, NFF, 128)
        gu = np.concatenate([g, u], axis=3)
        out["wf%di%d" % (i, l)] = np.ascontiguousarray(gu.transpose(2, 1, 0, 3))
        wo = inp[nm + "_out"][l]
        out["wf%do%d" % (i, l)] = np.ascontiguousarray(wo.reshape(NFF, 128, 8, 128).transpose(2, 1, 0, 3))
    w = inp["w_in"][l]
    cols = []
    pair = lambda base: [np.r_[base + g * 64:base + g * 64 + 64, base + (g + 4) * 64:base + (g + 4) * 64 + 64] for g in range(4)]
    cols += pair(0)
    cols += [np.arange(512, 640)]
    cols += pair(768)
    cols += [np.arange(1280, 1408)]
    cols += [np.arange(1536 + h * 128, 1536 + (h + 1) * 128) for h in range(4)]
    cols += [np.arange(2048 + k * 128, 2048 + (k + 1) * 128) for k in range(2)]
    cols += [np.arange(2560 + c * 128, 2560 + (c + 1) * 128) for c in range(24)]
    wfm = np.stack([w[:, c] for c in cols], axis=0)
    out["wfm%d" % l] = np.ascontiguousarray(wfm.reshape(40, 8, 128, 128).transpose(0, 2, 1, 3))
    vcols = np.r_[640:768, 1408:1536, 2304:2560]
    out["wv%d" % l] = np.ascontiguousarray(w[:, vcols].reshape(8, 128, 512).transpose(1, 0, 2))
    wb = inp["w_branch"][l]
    prow = np.concatenate([np.r_[g * 64:g * 64 + 64, (g + 4) * 64:(g + 4) * 64 + 64] for g in range(4)])
    wb2 = np.stack([wb[0][prow], wb[1][prow], wb[2]], axis=0)
    out["wbr%d" % l] = np.ascontiguousarray(wb2.reshape(3, 4, 128, 8, 128).transpose(3, 2, 0, 1, 4).reshape(8, 128, 12, 128))
    wo = inp["w_out"][l]
    out["wo%d" % l] = np.ascontiguousarray(wo.reshape(8, 128, 8, 128).transpose(2, 1, 0, 3))
    vec = np.zeros((128, 320), np.float32)
    vec[:, 0:8] = inp["norm_ffn1"][l].reshape(8, 128).T
    vec[:, 8:16] = inp["norm_mix"][l].reshape(8, 128).T
    vec[:, 16:24] = inp["norm_ffn2"][l].reshape(8, 128).T
    vec[:, 24] = np.tile(inp["qnorm_a"][l], 2)
    vec[:, 25] = np.tile(inp["knorm_a"][l], 2)
    vec[:, 26] = inp["subln_c"][l]
    vec[:, 32:40] = inp["sink_b"][l][None, :]
    vec[:, 40:104] = inp["lam_q1"][l][None, :]
    vec[:, 104:168] = inp["lam_k1"][l][None, :]
    vec[:, 168:232] = inp["lam_q2"][l][None, :]
    vec[:, 232:296] = inp["lam_k2"][l][None, :]
    out["vec%d" % l] = vec
    return out


def prep_bias(rel_bias, T):
    NT = T // 128
    k = np.arange(128)[:, None]
    u = np.arange(1152)[None, :] - 512
    bc = _t5_bucket_np(k - u)
    tab_c = rel_bias[:, 8:12]
    gc = np.ascontiguousarray(np.stack([tab_c[:, h][bc] for h in range(4)], axis=0)).astype(np.float32)
    left = tab_c[15]
    right = tab_c[31]
    per_rank = []
    q = np.arange(128)[None, :]
    tab_b = rel_bias[:, 0:8]
    btiles = np.zeros((2, 3, 128, 4, 128), np.float32)
    for dl in (-1, 0, 1):
        rel = 128 * dl + k - q
        bk = _t5_bucket_np(rel)
        ok = np.abs(rel) <= 128
        for kv in range(2):
            for g in range(4):
                v = tab_b[:, kv * 4 + g][bk]
                btiles[kv, dl + 1, :, g, :] = np.where(ok, v, np.float32(NEG))
    for r in range(4):
        ce = np.zeros((4, 2, 128, 512), np.float32)
        cb = np.zeros((128, 20), np.float32)
        for h in range(4):
            ce[h, 0] = gc[h][:, 512 + 128:512 + 128 + 512] if r > 0 else right[h]
            ce[h, 1] = gc[h][:, 0:512] if r < 3 else left[h]
            cb[:, h * 5 + 0] = left[h]
            cb[:, h * 5 + 1] = right[h]
            for s in range(1, 4):
                cb[:, h * 5 + 1 + s] = right[h] if ((r + s) % 4) > r else left[h]
        bb = np.zeros((2, 5, 128, 512), np.float32)
        for kv in range(2):
            for v in range(3):
                bb[kv, v] = btiles[kv, v].reshape(128, 512)
            bb[kv, 3] = bb[kv, 0] if r > 0 else np.float32(NEG)
            bb[kv, 4] = bb[kv, 2] if r < 3 else np.float32(NEG)
        per_rank.append({"cedge": ce, "cb": cb, "bb": np.ascontiguousarray(bb.reshape(10, 128, 512))})
    return gc, per_rank


def rotate_gather(shards, b, r, axis):
    return np.concatenate([shards[b * 4 + (r + j) % 4] for j in range(4)], axis=axis)


_NC_CACHE = {}
DEBUG = None
FUSED = True
PRECONVERT = False


def run_model(inp, T):
    S = 4 * T
    x = np.asarray(inp["x"], np.float32)
    inp = {k: np.asarray(v, np.float32) for k, v in inp.items()}
    assert x.shape == (2, S, D)
    cos, sin = _rope_tables(S)
    gc, per_rank = prep_bias(inp["rel_bias"], T)
    rmat = _rmat()
    lw = {}
    for l in range(2):
        lw.update(prep_layer_weights(inp, l))
    gfin = np.ascontiguousarray(inp["norm_final"].reshape(8, 128).T)

    core_static = []
    for c in range(NCORES):
        b, r = c // 4, c % 4
        sl = slice(r * T, (r + 1) * T)
        cs = np.stack([np.tile(cos[sl].T, (2, 1)), np.tile(sin[sl].T, (2, 1))], axis=1)
        dct = {"cs": np.ascontiguousarray(cs), "gc": gc, "rmat": rmat, "ident": np.eye(128, dtype=np.float32)}
        dct.update(per_rank[r])
        core_static.append(dct)

    def wnames(l, d1, d2):
        n = []
        if d1:
            n += ["wf1i%d" % l, "wf1o%d" % l, "wfm%d" % l, "wv%d" % l]
        if d2:
            n += ["wf2i%d" % l, "wf2o%d" % l, "wbr%d" % l, "wo%d" % l]
        return n

    def wnames_all():
        return wnames(0, True, True) + wnames(1, True, True)

    def launch(key, stages, ext_in, ext_out, final, maps):
        ck = (key, T)
        if ck not in _NC_CACHE:
            _NC_CACHE[ck] = Builder(T, stages, ext_in, ext_out, final).build()
        nc = _NC_CACHE[ck]
        res = run_bass_kernel_spmd(nc, maps, core_ids=list(range(NCORES)))
        if DEBUG is not None:
            DEBUG[key] = res.results
            DEBUG[key + "_in"] = maps
        return res.results

    if FUSED:
        ext_in = ["xin0", "cs", "rmat", "ident", "vec0", "vec1", "gfin", "gc", "cedge", "cb", "bb"] + wnames_all()
        maps = []
        for c in range(NCORES):
            b, r = c // 4, c % 4
            xT = np.ascontiguousarray(x[b, r * T:(r + 1) * T, :].T.reshape(8, 128, T).transpose(1, 0, 2))
            m = {"xin0": xT, "vec0": lw["vec0"], "vec1": lw["vec1"], "gfin": gfin}
            for n in ("cs", "gc", "cedge", "cb", "bb", "rmat", "ident"):
                m[n] = core_static[c][n]
            for n in wnames_all():
                m[n] = lw[n]
            maps.append(m)
        stages = [("d1", 0), ("ag", 0), ("attn", 0), ("d2", 0), ("d1", 1), ("ag", 1), ("attn", 1), ("d2", 1)]
        ck = ("F", T)
        if ck not in _NC_CACHE:
            _NC_CACHE[ck] = Builder(T, stages, ext_in, ["outT"], True, fused=True).build()
        res = run_bass_kernel_spmd(_NC_CACHE[ck], maps, core_ids=list(range(NCORES))).results
        out = np.zeros((2, S, D), np.float32)
        for c in range(NCORES):
            b, r = c // 4, c % 4
            oT = np.asarray(res[c]["outT"], np.float32)
            out[b, r * T:(r + 1) * T, :] = oT.transpose(1, 0, 2).reshape(D, T).T
        return out

    hand = ["xres", "qa", "qb", "qc", "gt"]
    own = ["kao", "kbo", "kco", "vo"]

    ext_in = ["xin0", "cs", "rmat", "vec0"] + wnames(0, True, False)
    ext_out = [h + "0" for h in hand + own]
    maps = []
    for c in range(NCORES):
        b, r = c // 4, c % 4
        xT = np.ascontiguousarray(x[b, r * T:(r + 1) * T, :].T.reshape(8, 128, T).transpose(1, 0, 2))
        m = {"xin0": xT, "cs": core_static[c]["cs"], "rmat": rmat, "vec0": lw["vec0"]}
        for n in wnames(0, True, False):
            m[n] = lw[n]
        maps.append(m)
    res = launch("L1", [("d1", 0)], ext_in, ext_out, False, maps)

    def attn_inputs(res, l):
        mlist = []
        for c in range(NCORES):
            b, r = c // 4, c % 4
            m = {}
            for h in hand:
                m[h + "%d" % l] = res[c][h + "%d" % l]
            m["kaf%d" % l] = rotate_gather([rr["kao%d" % l] for rr in res], b, r, 1)
            m["kbf%d" % l] = rotate_gather([rr["kbo%d" % l] for rr in res], b, r, 1)
            m["kcf%d" % l] = rotate_gather([rr["kco%d" % l] for rr in res], b, r, 2)
            m["vf%d" % l] = rotate_gather([rr["vo%d" % l] for rr in res], b, r, 0)
            for n in ("gc", "cedge", "cb", "bb", "rmat"):
                m[n] = core_static[c][n]
            mlist.append(m)
        return mlist

    maps = attn_inputs(res, 0)
    ext_in = list(maps[0].keys()) + ["vec0", "vec1", "cs"] + wnames(0, False, True) + wnames(1, True, False)
    for c in range(NCORES):
        maps[c]["vec0"] = lw["vec0"]
        maps[c]["vec1"] = lw["vec1"]
        maps[c]["cs"] = core_static[c]["cs"]
        for n in wnames(0, False, True) + wnames(1, True, False):
            maps[c][n] = lw[n]
    ext_out = [h + "1" for h in hand + own]
    if DEBUG is not None:
        ext_out += ["yT0", "xin1"]
    res = launch("L2", [("attn", 0), ("d2", 0), ("d1", 1)], ext_in, ext_out, False, maps)

    maps = attn_inputs(res, 1)
    ext_in = list(maps[0].keys()) + ["vec1", "gfin"] + wnames(1, False, True)
    for c in range(NCORES):
        maps[c]["vec1"] = lw["vec1"]
        maps[c]["gfin"] = gfin
        for n in wnames(1, False, True):
            maps[c][n] = lw[n]
    res = launch("L3", [("attn", 1), ("d2", 1)], ext_in, ["outT"], True, maps)

    out = np.zeros((2, S, D), np.float32)
    for c in range(NCORES):
        b, r = c // 4, c % 4
        oT = np.asarray(res[c]["outT"], np.float32)
        out[b, r * T:(r + 1) * T, :] = oT.transpose(1, 0, 2).reshape(D, T).T
    return out


def kernel(**inputs):
    return run_model(inputs, 4096)
```
